# Optimizing a Trainium2 kernel written in Bass

```python
import math
import jax, jax.numpy as jnp
from jax import lax
import numpy as np

D_MODEL = 2048
BATCH = 4
SEQ = 8192
DEPTH = 1

N_META = 16
BLOCK = 128
PAD = BLOCK - N_META
HG_HEADS = 8
HG_DK = 128
HG_DV = 128
HG_CHUNK = 64
SB_HEADS = 8
SB_DH = 128
D_FF = 5632
LN_EPS = 1e-5
RMS_EPS = 1e-6
DN_ALPHA = (2.0 * DEPTH) ** 0.25
DN_BETA = (8.0 * DEPTH) ** -0.25

HG_QK_W = HG_HEADS * HG_DK
HG_V_W = HG_HEADS * HG_DV
SB_W = SB_HEADS * SB_DH
SPLIT_IDX = (HG_QK_W, 2 * HG_QK_W, 2 * HG_QK_W + HG_V_W, 2 * HG_QK_W + 2 * HG_V_W,
             2 * HG_QK_W + 2 * HG_V_W + SB_W, 2 * HG_QK_W + 2 * HG_V_W + 2 * SB_W,
             2 * HG_QK_W + 2 * HG_V_W + 3 * SB_W)
IN_COLS = SPLIT_IDX[-1] + 2 * D_MODEL

kernel_name = "hybrid_hgrn2_stickbreaking_macaron_deepnorm"


def layer_norm(x, g, b):
    xf = x.astype(jnp.float32)
    mu = jnp.mean(xf, axis=-1, keepdims=True)
    var = jnp.mean(jnp.square(xf - mu), axis=-1, keepdims=True)
    return ((xf - mu) * lax.rsqrt(var + LN_EPS) * g.astype(jnp.float32) + b.astype(jnp.float32)).astype(x.dtype)


def swiglu(x, w_gate, w_up, w_down):
    return (jax.nn.silu(x @ w_gate) * (x @ w_up)) @ w_down


def hgrn2_chunked(q, k, v, log_f):
    B, L, H, DK = q.shape
    DV = v.shape[-1]
    n = L // HG_CHUNK

    def to_chunks(t):
        return t.astype(jnp.float32).reshape(B, n, HG_CHUNK, H, t.shape[-1]).transpose(1, 0, 3, 2, 4)

    qc, kc, vc, gc = to_chunks(q), to_chunks(k), to_chunks(v), to_chunks(log_f)
    causal = jnp.tril(jnp.ones((HG_CHUNK, HG_CHUNK), dtype=bool))[None, None, :, :, None]

    def step(S, inp):
        qi, ki, vi, gi = inp
        b = jnp.cumsum(gi, axis=2)
        o_inter = jnp.einsum('bhtk,bhkv->bhtv', qi * jnp.exp(b), S)
        rel = jnp.where(causal, b[:, :, :, None, :] - b[:, :, None, :, :], -jnp.inf)
        scores = jnp.einsum('bhtk,bhsk,bhtsk->bhts', qi, ki, jnp.exp(rel))
        o_intra = jnp.einsum('bhts,bhsv->bhtv', scores, vi)
        b_last = b[:, :, -1:, :]
        S_new = jnp.exp(b_last[:, :, 0, :])[..., None] * S + jnp.einsum(
            'bhsk,bhsv->bhkv', ki * jnp.exp(b_last - b), vi)
        return S_new, o_inter + o_intra

    S0 = jnp.zeros((B, H, DK, DV), jnp.float32)
    _, o = lax.scan(step, S0, (qc, kc, vc, gc))
    return o.transpose(1, 0, 3, 2, 4).reshape(B, L, H, DV)


def stick_breaking(q, k, v, key_valid):
    B, L, H, D = q.shape
    nb = L // BLOCK
    scale = 1.0 / math.sqrt(D)
    qb = q.astype(jnp.float32).reshape(B, nb, BLOCK, H, D).transpose(1, 0, 3, 2, 4)
    kf = k.astype(jnp.float32)
    vf = v.astype(jnp.float32)
    kpos = jnp.arange(L)

    def one_block(args):
        qi, start = args
        z = jnp.einsum('bhtd,bshd->bhts', qi, kf) * scale
        qpos = start + jnp.arange(BLOCK)
        mask = (kpos[None, :] < qpos[:, None]) & key_valid[None, :]
        log_beta = jax.nn.log_sigmoid(z)
        log_1mb = jnp.where(mask, jax.nn.log_sigmoid(-z), 0.0)
        suffix = jnp.flip(jnp.cumsum(jnp.flip(log_1mb, axis=-1), axis=-1), axis=-1)
        w = jnp.where(mask, jnp.exp(log_beta + suffix - log_1mb), 0.0)
        return jnp.einsum('bhts,bshd->bhtd', w, vf)

    starts = jnp.arange(nb) * BLOCK
    o = lax.map(one_block, (qb, starts))
    return o.transpose(1, 0, 3, 2, 4).reshape(B, L, H, D)


def gated_mixer(h, valid, w_in, b_gate, lb, hg_norm_g, w_proj_hg, w_proj_sb, w_out):
    B, L, _ = h.shape
    proj = h @ w_in
    hq, hf, hi, hog, sq, sk, sv, gates = jnp.split(proj, SPLIT_IDX, axis=-1)
    vmask = valid[:, None]

    f = lb + (1.0 - lb) * jax.nn.sigmoid(hf.astype(jnp.float32))
    log_f = jnp.where(vmask, jnp.log(f), 0.0)
    k_hg = jnp.where(vmask, 1.0 - f, 0.0)
    q_hg = jax.nn.silu(hq.astype(jnp.float32))
    o_hg = hgrn2_chunked(q_hg.reshape(B, L, HG_HEADS, HG_DK), k_hg.reshape(B, L, HG_HEADS, HG_DK),
                         hi.reshape(B, L, HG_HEADS, HG_DV), log_f.reshape(B, L, HG_HEADS, HG_DK))
    o_hg = o_hg * lax.rsqrt(jnp.mean(jnp.square(o_hg), axis=-1, keepdims=True) + RMS_EPS)
    o_hg = o_hg * hg_norm_g.astype(jnp.float32).reshape(HG_HEADS, HG_DV)
    o_hg = (o_hg.reshape(B, L, HG_V_W) * jax.nn.silu(hog.astype(jnp.float32))).astype(h.dtype)

    o_sb = stick_breaking(sq.reshape(B, L, SB_HEADS, SB_DH), sk.reshape(B, L, SB_HEADS, SB_DH),
                          sv.reshape(B, L, SB_HEADS, SB_DH), valid)
    o_sb = o_sb.reshape(B, L, SB_W).astype(h.dtype)

    g = jax.nn.sigmoid((gates + b_gate).astype(jnp.float32)).astype(h.dtype)
    g_hg, g_sb = jnp.split(g, 2, axis=-1)
    y = g_hg * (o_hg @ w_proj_hg) + g_sb * (o_sb @ w_proj_sb)
    return y @ w_out


def setup_inputs(seed: int = 0) -> dict:
    key = jax.random.key(seed)
    ks = jax.random.split(key, 24)
    f32 = jnp.float32

    def nrm(k, shape, scale):
        return jax.random.normal(k, shape, f32) * scale

    def gain(k, shape):
        return 1.0 + 0.02 * jax.random.normal(k, shape, f32)

    d_inv = D_MODEL ** -0.5
    return {
        "x": nrm(ks[0], (BATCH, SEQ, D_MODEL), 1.0),
        "meta": nrm(ks[1], (N_META, D_MODEL), 1.0),
        "ln1_g": gain(ks[2], (DEPTH, D_MODEL)),
        "ln1_b": nrm(ks[3], (DEPTH, D_MODEL), 0.02),
        "ffn1_w_gate": nrm(ks[4], (DEPTH, D_MODEL, D_FF), d_inv),
        "ffn1_w_up": nrm(ks[5], (DEPTH, D_MODEL, D_FF), d_inv),
        "ffn1_w_down": nrm(ks[6], (DEPTH, D_FF, D_MODEL), D_FF ** -0.5 * DN_BETA),
        "w_in": nrm(ks[7], (DEPTH, D_MODEL, IN_COLS), d_inv),
        "b_gate": nrm(ks[8], (DEPTH, 2 * D_MODEL), 0.1),
        "hg_lb_logits": nrm(ks[9], (DEPTH + 1, HG_QK_W), 0.1),
        "hg_norm_g": gain(ks[10], (DEPTH, HG_V_W)),
        "w_proj_hg": nrm(ks[11], (DEPTH, HG_V_W, D_MODEL), HG_V_W ** -0.5),
        "w_proj_sb": nrm(ks[12], (DEPTH, SB_W, D_MODEL), SB_W ** -0.5),
        "w_out": nrm(ks[13], (DEPTH, D_MODEL, D_MODEL), d_inv * DN_BETA),
        "ln2_g": gain(ks[14], (DEPTH, D_MODEL)),
        "ln2_b": nrm(ks[15], (DEPTH, D_MODEL), 0.02),
        "ffn2_w_gate": nrm(ks[16], (DEPTH, D_MODEL, D_FF), d_inv),
        "ffn2_w_up": nrm(ks[17], (DEPTH, D_MODEL, D_FF), d_inv),
        "ffn2_w_down": nrm(ks[18], (DEPTH, D_FF, D_MODEL), D_FF ** -0.5 * DN_BETA),
        "ln3_g": gain(ks[19], (DEPTH, D_MODEL)),
        "ln3_b": nrm(ks[20], (DEPTH, D_MODEL), 0.02),
    }


def reference(x, meta, ln1_g, ln1_b, ffn1_w_gate, ffn1_w_up, ffn1_w_down, w_in, b_gate, hg_lb_logits,
              hg_norm_g, w_proj_hg, w_proj_sb, w_out, ln2_g, ln2_b, ffn2_w_gate, ffn2_w_up, ffn2_w_down,
              ln3_g, ln3_b):
    B, S, D = x.shape
    pad = jnp.zeros((B, PAD, D), x.dtype)
    meta_b = jnp.broadcast_to(meta.astype(x.dtype)[None], (B, N_META, D))
    h = jnp.concatenate([pad, meta_b, x], axis=1)
    L = h.shape[1]
    valid = jnp.arange(L) >= PAD

    lb_all = jnp.cumsum(jax.nn.softmax(hg_lb_logits.astype(jnp.float32), axis=0), axis=0)

    for l in range(DEPTH):
        h = layer_norm(DN_ALPHA * h + 0.5 * swiglu(h, ffn1_w_gate[l], ffn1_w_up[l], ffn1_w_down[l]),
                       ln1_g[l], ln1_b[l])
        mix = gated_mixer(h, valid, w_in[l], b_gate[l], lb_all[l], hg_norm_g[l],
                          w_proj_hg[l], w_proj_sb[l], w_out[l])
        h = layer_norm(DN_ALPHA * h + mix, ln2_g[l], ln2_b[l])
        h = layer_norm(DN_ALPHA * h + 0.5 * swiglu(h, ffn2_w_gate[l], ffn2_w_up[l], ffn2_w_down[l]),
                       ln3_g[l], ln3_b[l])

    return h[:, PAD + N_META:]
```

```python
import math
from contextlib import ExitStack

import numpy as np
import concourse.bass as bass
import concourse.mybir as mybir
from concourse.bass_utils import run_bass_kernel_spmd

F32 = mybir.dt.float32
BF16 = mybir.dt.bfloat16
AF = mybir.ActivationFunctionType
ALU = mybir.AluOpType

D = 2048
DC = 16
FF = 5632
FC = 44
NH = 8
ALPHA = 2.0 ** 0.25
LN_EPS = 1e-5
RMS_EPS = 1e-6
NMETA = 16
PADN = 112
ENGS = ["pe", "act", "dve", "pool", "sp"]
NDS = 32
CH = 16000
NCHUNK = 8


class Sched:
    def __init__(self, nc, st):
        self.nc = nc
        self.ops = {e: [] for e in ENGS}
        self.cnt = {e: 0 for e in ENGS}
        self.res_w = {}
        self.res_r = {}
        self.waited = {e: {} for e in ENGS}
        self.ndma = 0
        self.ndma_q = [0, 0]
        self.dma_last = [0] * NDS
        self.esem = {e: [st.enter_context(nc.semaphore(f"s_{e}{i}")) for i in range(NCHUNK)] for e in ENGS[:4]}
        self.dsem = [st.enter_context(nc.semaphore(f"s_d{i}")) for i in range(NDS)]
        self.nops = 0

    def _need(self, eng, tok, waits):
        if tok[0] == "e":
            if tok[1] == eng and eng == "pe":
                return
            key = ("e", tok[1])
        else:
            key = ("d", tok[1])
        if self.waited[eng].get(key, 0) >= tok[2]:
            return
        self.waited[eng][key] = tok[2]
        waits.append(tok)

    def op(self, eng, fn, r=(), w=(), sig=True, dma=False):
        deps = []
        for k in r:
            t = self.res_w.get(k)
            if t is not None:
                deps.append(t)
        for k in w:
            t = self.res_w.get(k)
            if t is not None:
                deps.append(t)
            rr = self.res_r.get(k)
            if rr:
                for kk, v in rr.items():
                    if kk == "dma":
                        deps.extend(v)
                    else:
                        deps.append(v)
        waits = []
        for t in deps:
            self._need(eng, t, waits)
        if dma:
            half = NDS // 2
            qi = 0 if eng == "sp" else 1
            n = self.ndma_q[qi]
            self.ndma_q[qi] += 1
            k = qi * half + n % half
            if self.dma_last[k] > 0:
                self._need(eng, ("d", k, self.dma_last[k]), waits)
            v = self.dma_last[k] + 16
            self.ndma += 1
            self.dma_last[k] = v
            tok = ("d", k, v)
        else:
            if sig:
                self.cnt[eng] += 1
                tok = ("e", eng, self.cnt[eng])
            else:
                tok = ("e", eng, self.cnt[eng] + 1)
        for k in w:
            self.res_w[k] = tok
            self.res_r[k] = {}
        for k in r:
            d = self.res_r.setdefault(k, {})
            if dma:
                d.setdefault("dma", []).append(tok)
            else:
                d[eng] = tok
        self.ops[eng].append((waits, fn, tok if (sig or dma) else None))
        self.nops += 1

    def barrier(self):
        for e in ENGS:
            waits = []
            for e2 in ENGS[:4]:
                if e2 != e and self.cnt[e2] > 0:
                    self._need(e, ("e", e2, self.cnt[e2]), waits)
            for k in range(NDS):
                if self.dma_last[k] > 0:
                    self._need(e, ("d", k, self.dma_last[k]), waits)
            if waits:
                self.ops[e].append((waits, None, None))

    def _semval(self, tok):
        if tok[0] == "e":
            c = (tok[2] - 1) // CH
            assert c < NCHUNK, "semaphore chunks exhausted"
            return self.esem[tok[1]][c], (tok[2] - 1) % CH + 1
        return self.dsem[tok[1]], tok[2]

    def emit(self):
        nc = self.nc
        for e in ENGS[:4]:
            for (waits, fn, tok) in self.ops[e]:
                for t in waits:
                    if t[0] == "e":
                        assert t[2] <= self.cnt[t[1]], ("unsignaled dependency", e, t)
        with nc.Block() as block:
            def runner(name):
                def f(eng):
                    for (waits, fn, tok) in self.ops[name]:
                        for t in waits:
                            s, v = self._semval(t)
                            eng.wait_ge(s, v)
                        if fn is None:
                            continue
                        ins = fn(eng)
                        if tok is not None:
                            if tok[0] == "d":
                                ins.then_inc(self.dsem[tok[1]], 16)
                            else:
                                s, _ = self._semval(tok)
                                ins.then_inc(s, 1)
                return f
            block.sync(runner("sp"))
            block.tensor(runner("pe"))
            block.scalar(runner("act"))
            block.vector(runner("dve"))
            block.gpsimd(runner("pool"))
        for e in ENGS:
            self.ops[e] = []

    def final_wait(self):
        waits = []
        for k in range(NDS):
            if self.dma_last[k] > 0:
                self._need("sp", ("d", k, self.dma_last[k]), waits)
        for e2 in ENGS[:4]:
            if self.cnt[e2] > 0:
                self._need("sp", ("e", e2, self.cnt[e2]), waits)
        if waits:
            self.ops["sp"].append((waits, None, None))


class K:
    def __init__(self, S):
        self.S = S

    def mm(self, out, lhsT, rhs, start, stop, r, w, sig):
        self.S.op("pe", lambda e: e.matmul(out, lhsT, rhs, start=start, stop=stop), r, w, sig=sig)

    def act(self, out, in_, func, r, w, bias=None, scale=None):
        kw = {}
        if bias is not None:
            kw["bias"] = bias
        if scale is not None:
            kw["scale"] = scale
        self.S.op("act", lambda e: e.activation(out, in_, func, **kw), r, w)

    def tt(self, eng, out, in0, in1, op, r, w):
        self.S.op(eng, lambda e: e.tensor_tensor(out, in0, in1, op), r, w)

    def ts(self, eng, out, in0, s1, s2, op0, op1, r, w):
        if op1 is None:
            self.S.op(eng, lambda e: e.tensor_scalar(out, in0, s1, None, op0), r, w)
        else:
            self.S.op(eng, lambda e: e.tensor_scalar(out, in0, s1, s2, op0, op1), r, w)

    def stt(self, eng, out, in0, scalar, in1, op0, op1, r, w):
        self.S.op(eng, lambda e: e.scalar_tensor_tensor(out, in0, scalar, in1, op0, op1), r, w)

    def recip(self, out, in_, r, w):
        self.S.op("dve", lambda e: e.reciprocal(out, in_), r, w)

    def cp(self, eng, out, in_, r, w):
        if eng == "act":
            self.S.op(eng, lambda e: e.copy(out, in_), r, w)
        else:
            self.S.op(eng, lambda e: e.tensor_copy(out, in_), r, w)

    def ms(self, eng, ap, val, w):
        self.S.op(eng, lambda e: e.memset(ap, val), (), w)

    def dma(self, q, out, in_, r, w):
        self.S.op(q, lambda e: e.dma_start(out=out, in_=in_), r, w, dma=True)


def cdiv(a, b):
    return (a + b - 1) // b


def build_program(SEQ, debug=False):
    L = SEQ + 128
    NB = L // 128
    assert SEQ % 512 == 0
    NT = SEQ // 512
    assert NT % 2 == 0
    NTO = NT // 2
    OWN0 = 128 + 512 * NTO
    OWNB = OWN0 // 128
    OWNT = 1 + NTO
    SO = SEQ // 2
    tilesA = [(0, 128)] + [(128 + 512 * i, 512) for i in range(NT)]
    tilesC = [(512 * i, 512) for i in range(NTO)]

    nc = bass.Bass("TRN2", target_bir_lowering=False)

    def din(name, shape, dt=F32):
        return nc.dram_tensor(name, list(shape), dt, kind="ExternalInput").ap()

    def dscr(name, shape, dt):
        return nc.dram_tensor(name, list(shape), dt).ap()

    xT = din("xT", [DC, 128, L])
    constsA = din("constsA", [128, 768])
    constsB = din("constsB", [128, 4352])
    lnp = din("lnp", [128, 6 * DC])
    bgate = din("bgate", [128, 32])
    gnorm = din("gnorm", [128, NH])
    lbl_fm = din("lbl_fm", [128, 2 * NH])
    lbl_tm = din("lbl_tm", [128, 2 * 1024])
    vm_tm_d = din("vm_tm", [128, NB])
    vm_fm_d = din("vm_fm", [128, L])
    Wi = {}
    for nm in ("f1", "f2"):
        Wi[nm] = dict(g32=din(f"{nm}_g", [FC, 128, DC * 128]), u32=din(f"{nm}_u", [FC, 128, DC * 128]),
                      d32=din(f"{nm}_d", [DC, 128, FC * 128]),
                      gbf=dscr(f"{nm}_gbf", [FC, 128, DC * 128], BF16), ubf=dscr(f"{nm}_ubf", [FC, 128, DC * 128], BF16),
                      dbf=dscr(f"{nm}_dbf", [DC, 128, FC * 128], BF16))
    win_fm32 = din("win_fm", [72, 128, DC * 128])
    win_tm32 = din("win_tm", [6, 128, DC * 512])
    win_fmbf = dscr("win_fmbf", [72, 128, DC * 128], BF16)
    win_tmbf = dscr("win_tmbf", [6, 128, DC * 512], BF16)
    phg32 = din("phg", [DC, 128, 8 * 128])
    psb32 = din("psb", [DC, 128, 8 * 128])
    wout32 = din("wout", [DC, 128, DC * 128])
    phgbf = dscr("phgbf", [DC, 128, 8 * 128], BF16)
    psbbf = dscr("psbbf", [DC, 128, 8 * 128], BF16)
    woutbf = dscr("woutbf", [DC, 128, DC * 128], BF16)

    outT = nc.dram_tensor("outT", [DC, 128, SO], F32, kind="ExternalOutput").ap()

    kind_dbg = dict(kind="ExternalOutput") if debug else {}

    def dscr2(name, shape, dt):
        return nc.dram_tensor(name, list(shape), dt, **kind_dbg).ap()

    h1T = dscr2("h1T", [DC, 128, L], F32)
    h1Tbf = dscr("h1Tbf", [DC, 128, L], BF16)
    qT = dscr("qT", [NH, 128, L], BF16)
    ogT = dscr("ogT", [NH, 128, L], BF16)
    omfT = dscr("omfT", [NH, 128, L], F32)
    sqT = dscr("sqT", [NH, 128, L], BF16)
    skT = dscr("skT", [NH, 128, L], BF16)
    gT = dscr("gT", [32, 128, L], BF16)
    logf = dscr("logf", [L, 1024], F32)
    omf = dscr("omf", [L, 1024], F32)
    vhg = dscr("vhg", [L, 1024], BF16)
    svv = dscr("svv", [L, 1024], BF16)
    ohgT = dscr2("ohgT", [NH, 128, L], BF16)
    osbT = dscr2("osbT", [NH, 128, L], BF16)
    h2T = dscr2("h2T", [DC, 128, SO], F32)
    h2Tbf = dscr("h2Tbf", [DC, 128, SO], BF16)

    top = ExitStack()
    with top:
        top.enter_context(nc.allow_low_precision("bf16 matmul operands with fp32 accumulation"))
        top.enter_context(nc.allow_non_contiguous_dma("strided tile loads"))
        S = Sched(nc, top)
        k = K(S)
        ps = [top.enter_context(nc.psum_tensor(f"ps{i}", [128, 512], F32)) for i in range(8)]

        def psb_(n, shp, dt):
            return top.enter_context(nc.sbuf_tensor(n, shp, dt))

        c32 = psb_("c32", [128, 768], F32)
        lnp_s = psb_("lnp_s", [128, 6 * DC], F32)
        bg_s = psb_("bg_s", [128, 32], F32)
        gn_s = psb_("gn_s", [128, NH], F32)
        lfm = psb_("lfm", [128, 2 * NH], F32)
        oml_fm = psb_("oml_fm", [128, NH], F32)
        k.dma("sp", c32[:], constsA, [], ["c32"])
        k.dma("sp", lnp_s[:], lnp, [], ["lnp"])
        k.dma("sp", bg_s[:], bgate, [], ["bg"])
        k.dma("sp", gn_s[:], gnorm, [], ["gn"])
        k.dma("sp", lfm[:], lbl_fm, [], ["lfm"])
        ones32 = c32[:, 0:128]
        onesb_t = psb_("onesb", [128, 128], BF16)
        k.ms("dve", onesb_t[:], 1.0, ["onesb"])
        onesB = onesb_t[:]
        SU64 = c32[0:64, 128:192]
        TRI64 = c32[0:64, 192:256]
        MASKLE = c32[0:64, 256:768]
        k.tt("dve", oml_fm[:], lfm[:, NH:2 * NH], lfm[:, 0:NH], ALU.subtract, ["lfm"], ["omlfm"])
        k.act(oml_fm[:], oml_fm[:], AF.Sigmoid, ["omlfm"], ["omlfm"])

        converted = set()

        def fetch(dst, dkey, src32, scr, skey, first, wst, n):
            first = skey not in converted
            converted.add(skey)
            if first:
                ceng = fetch.rot[fetch.slab % len(fetch.rot)]
                fetch.slab += 1
                for pi, p0 in enumerate(range(0, n, 2048)):
                    m = min(2048, n - p0)
                    j = fetch.ctr % len(wst)
                    fetch.ctr += 1
                    k.dma("sp", wst[j][:, :m], src32[:, p0:p0 + m], [], [f"wst{j}"])
                    k.cp(ceng, dst[:, p0:p0 + m], wst[j][:, :m], [f"wst{j}"], [dkey])
                k.dma("pool", scr, dst[:, :n], [dkey], [skey])
            else:
                k.dma("sp", dst[:, :n], scr, [skey], [dkey])
        fetch.ctr = 0
        fetch.slab = 0
        fetch.rot = ["act", "pool", "act", "dve"]

        def ln_finish(r, T, eps_eff, g_ap, b_ap, mean, msq, rstd, t1, yo, yb, out32_fn, outbf_fn, okey_fn):
            k.act(mean[:, :T], ps[6][:, :T], AF.Copy, ["ps6"], ["mean"], scale=1.0 / D)
            k.tt("dve", msq[:, :T], mean[:, :T], mean[:, :T], ALU.mult, ["mean"], ["msq"])
            k.stt("dve", msq[:, :T], ps[7][:, :T], 1.0 / D, msq[:, :T], ALU.mult, ALU.subtract, ["ps7", "msq"], ["msq"])
            k.act(rstd[:, :T], msq[:, :T], AF.Sqrt, ["msq"], ["rstd"], bias=eps_eff)
            k.recip(rstd[:, :T], rstd[:, :T], ["rstd"], ["rstd"])
            def chunk(c):
                j = c % 2
                k.tt("dve", t1[j][:, :T], r[:, c, :T], mean[:, :T], ALU.subtract, [f"r{c}", "mean"], [f"t1{j}"])
                k.tt("dve", t1[j][:, :T], t1[j][:, :T], rstd[:, :T], ALU.mult, [f"t1{j}", "rstd"], [f"t1{j}"])
                k.ts("dve", yo[j][:, :T], t1[j][:, :T], g_ap[:, c:c + 1], b_ap[:, c:c + 1], ALU.mult, ALU.add,
                     [f"t1{j}", "lnp"], [f"yo{j}"])
                k.dma("pool", out32_fn(c), yo[j][:, :T], [f"yo{j}"], [okey_fn(c, 0)])
                if outbf_fn is not None:
                    k.cp("pool", yb[j][:, :T], yo[j][:, :T], [f"yo{j}"], [f"yb{j}"])
                    k.dma("pool", outbf_fn(c), yb[j][:, :T], [f"yb{j}"], [okey_fn(c, 1)])
            return [(lambda c=c: chunk(c)) for c in range(DC)]

        def ffn_phase(tag, tiles, x_src, x_is_f32, xkey, res_src, W, out32, outbf, okey, lg, lb_):
            S.barrier()
            with ExitStack() as st:
                def sb(n, shp, dt):
                    return st.enter_context(nc.sbuf_tensor(f"{tag}_{n}", shp, dt))
                xbfs = [sb(f"xbf{i}", [128, DC, 512], BF16) for i in range(2)]
                xp = [sb(f"xp{i}", [128, 512], F32) for i in range(2)]
                HT = sb("HT", [128, FC, 512], BF16)
                wst = [sb(f"wst{i}", [128, 2048], F32) for i in range(2)]
                wgb = [sb(f"wgb{i}", [128, DC * 128], BF16) for i in range(2)]
                wub = [sb(f"wub{i}", [128, DC * 128], BF16) for i in range(2)]
                wdb = [sb(f"wdb{i}", [128, FC * 128], BF16) for i in range(2)]
                xs = [sb(f"xs{i}", [128, 512], F32) for i in range(2)]
                sg = [sb(f"sg{i}", [128, 512], F32) for i in range(2)]
                r = sb("r", [128, DC, 512], F32)
                sqb = [sb(f"sqb{i}", [128, 512], BF16) for i in range(2)]
                rhi = [sb(f"rhi{i}", [128, 512], BF16) for i in range(2)]
                rlo = [sb(f"rlo{i}", [128, 512], BF16) for i in range(2)]
                mean = sb("mean", [128, 512], F32)
                msq = sb("msq", [128, 512], F32)
                rstd = sb("rstd", [128, 512], F32)
                t1 = [sb(f"t1{i}", [128, 512], F32) for i in range(2)]
                yo = [sb(f"yo{i}", [128, 512], F32) for i in range(2)]
                yb = [sb(f"yb{i}", [128, 512], BF16) for i in range(2)]
                def load_x(ti, oi):
                    t0, T = tiles[ti]
                    xb = xbfs[oi % 2]
                    p = oi % 2
                    if x_is_f32:
                        for c in range(DC):
                            j = c % 2
                            k.dma("sp", xp[j][:, :T], x_src[c, :, t0:t0 + T], [], [f"xp{j}"])
                            k.cp("dve", xb[:, c, :T], xp[j][:, :T], [f"xp{j}"], [f"xbf{p}_{c}"])
                    else:
                        for c0 in range(0, DC, 8):
                            k.dma("sp", xb[:, c0:c0 + 8, :T], x_src[c0:c0 + 8, :, t0:t0 + T].rearrange("c p t -> p c t"),
                                  [xkey(ti, c, 1) for c in range(c0, c0 + 8)], [f"xbf{p}_{c}" for c in range(c0, c0 + 8)])

                order = list(range(len(tiles)))
                if len(tiles) > 1 and tiles[0][1] < tiles[1][1]:
                    order[0], order[1] = 1, 0
                load_x(order[0], 0)
                pending = []
                for oi, ti in enumerate(order):
                    t0, T = tiles[ti]
                    first = oi == 0
                    xbf = xbfs[oi % 2]
                    xpar = oi % 2
                    for f in range(FC):
                        s = f % 2
                        if pending and f % 2 == 1:
                            pending.pop(0)()
                        fetch(wgb[s], f"wgb{s}", W["g32"][f], W["gbf"][f], f"{tag}g{f}", first, wst, DC * 128)
                        fetch(wub[s], f"wub{s}", W["u32"][f], W["ubf"][f], f"{tag}u{f}", first, wst, DC * 128)
                        pg, pu = ps[s], ps[2 + s]
                        for c in range(DC):
                            k.mm(pg[:, :T], wgb[s][:, c * 128:(c + 1) * 128], xbf[:, c, :T], c == 0, c == DC - 1,
                                 [f"wgb{s}", f"xbf{xpar}_{c}"], [f"ps{s}"], c == DC - 1)
                        for c in range(DC):
                            k.mm(pu[:, :T], wub[s][:, c * 128:(c + 1) * 128], xbf[:, c, :T], c == 0, c == DC - 1,
                                 [f"wub{s}", f"xbf{xpar}_{c}"], [f"ps{2 + s}"], c == DC - 1)
                        k.act(sg[s][:, :T], pg[:, :T], AF.Silu, [f"ps{s}"], [f"sg{s}"])
                        k.tt("dve", HT[:, f, :T], sg[s][:, :T], pu[:, :T], ALU.mult, [f"sg{s}", f"ps{2 + s}"], [f"HT{f}"])
                    while pending:
                        pending.pop(0)()
                    if oi + 1 < len(order):
                        load_x(order[oi + 1], oi + 1)

                    def stats(d, T=T):
                        z = d % 2
                        k.mm(ps[6][:, :T], onesB, rhi[z][:, :T], d == 0, False, [f"rhi{z}", "onesb"], ["ps6"], False)
                        k.mm(ps[6][:, :T], onesB, rlo[z][:, :T], False, d == DC - 1, [f"rlo{z}", "onesb"], ["ps6"], True)
                        k.mm(ps[7][:, :T], onesB, sqb[z][:, :T], d == 0, d == DC - 1, [f"sqb{z}", "onesb"], ["ps7"], True)

                    for dc in range(DC):
                        s = dc % 2
                        fetch(wdb[s], f"wdb{s}", W["d32"][dc], W["dbf"][dc], f"{tag}d{dc}", first, wst, FC * 128)
                        py = ps[4 + s]
                        for f in range(FC):
                            k.mm(py[:, :T], wdb[s][:, f * 128:(f + 1) * 128], HT[:, f, :T], f == 0, f == FC - 1,
                                 [f"wdb{s}", f"HT{f}"], [f"ps{4 + s}"], f == FC - 1)
                        if dc > 0:
                            stats(dc - 1)
                        k.dma("sp", xs[s][:, :T], res_src[dc, :, t0:t0 + T], [xkey(ti, dc, 0)], [f"xs{s}"])
                        k.stt("dve", r[:, dc, :T], xs[s][:, :T], 2.0 * ALPHA, py[:, :T], ALU.mult, ALU.add,
                              [f"xs{s}", f"ps{4 + s}"], [f"r{dc}"])
                        k.act(sqb[s][:, :T], r[:, dc, :T], AF.Square, [f"r{dc}"], [f"sqb{s}"])
                        k.cp("dve", rhi[s][:, :T], r[:, dc, :T], [f"r{dc}"], [f"rhi{s}"])
                        k.tt("dve", rlo[s][:, :T], r[:, dc, :T], rhi[s][:, :T], ALU.subtract, [f"r{dc}", f"rhi{s}"], [f"rlo{s}"])
                    stats(DC - 1)
                    pending = ln_finish(r, T, 4.0 * LN_EPS, lg, lb_, mean, msq, rstd, t1, yo, yb,
                                        lambda c, t0=t0, T=T: out32[c, :, t0:t0 + T],
                                        (lambda c, t0=t0, T=T: outbf[c, :, t0:t0 + T]) if outbf is not None else None,
                                        lambda c, b, ti=ti: okey(ti, c, b))
                while pending:
                    pending.pop(0)()
                S.barrier()
                S.emit()

        def phase_b1():
            S.barrier()
            with ExitStack() as st:
                def sb(n, shp, dt):
                    return st.enter_context(nc.sbuf_tensor(f"b1_{n}", shp, dt))
                xbf = sb("xbf", [128, DC, 512], BF16)
                wst = [sb(f"wst{i}", [128, 2048], F32) for i in range(3)]
                wfm = [sb(f"wfm{i}", [128, DC * 128], BF16) for i in range(2)]
                wtm = [sb(f"wtm{i}", [128, DC * 512], BF16) for i in range(2)]
                NE = 4
                e32 = [sb(f"e32{i}", [128, 512], F32) for i in range(NE)]
                ebf = [sb(f"ebf{i}", [128, 512], BF16) for i in range(NE)]
                f32b = [sb(f"f32b{i}", [128, 512], F32) for i in range(NE)]
                lf = [sb(f"lf{i}", [128, 512], F32) for i in range(NE)]
                om = [sb(f"om{i}", [128, 512], F32) for i in range(NE)]
                ltm = sb("ltm", [128, 2048], F32)
                oml_tm = sb("oml_tm", [128, 1024], F32)
                lb_tm = sb("lb_tm", [128, 1024], F32)
                k.dma("sp", ltm[:], lbl_tm, [], ["ltm"])
                k.tt("dve", oml_tm[:], ltm[:, 1024:2048], ltm[:, 0:1024], ALU.subtract, ["ltm"], ["omltm"])
                k.act(oml_tm[:], oml_tm[:], AF.Sigmoid, ["omltm"], ["omltm"])
                k.ts("dve", lb_tm[:], oml_tm[:], -1.0, 1.0, ALU.mult, ALU.add, ["omltm"], ["lbtm"])
                vmt = sb("vmt", [128, NB], F32)
                vmf = [sb(f"vmf{i}", [128, 512], F32) for i in range(2)]
                k.dma("sp", vmt[:], vm_tm_d, [], ["vmt"])
                qs = 1.0 / math.sqrt(128.0)
                ecnt = 0
                tmc = 0
                for ti, (t0, T) in enumerate(tilesA):
                    first = ti == 0
                    own = ti >= OWNT
                    for c0 in range(0, DC, 8):
                        k.dma("sp", xbf[:, c0:c0 + 8, :T], h1Tbf[c0:c0 + 8, :, t0:t0 + T].rearrange("c p t -> p c t"),
                              [f"h1_{ti}_{c}_1" for c in range(c0, c0 + 8)], ["xbf"])
                    jcnt = 0
                    for j in range(72):
                        if not own and (j // 8) != 4:
                            continue
                        s = jcnt % 2
                        jcnt += 1
                        fetch(wfm[s], f"wfm{s}", win_fm32[j], win_fmbf[j], f"winfm{j}", first, wst, DC * 128)
                        pp = ps[s]
                        for c in range(DC):
                            k.mm(pp[:, :T], wfm[s][:, c * 128:(c + 1) * 128], xbf[:, c, :T], c == 0, c == DC - 1,
                                 [f"wfm{s}", "xbf"], [f"ps{s}"], c == DC - 1)
                        e = ecnt % NE
                        ecnt += 1
                        grp, h = j // 8, j % 8
                        if grp == 0:
                            k.act(ebf[e][:, :T], pp[:, :T], AF.Silu, [f"ps{s}"], [f"ebf{e}"])
                            k.dma("pool", qT[h, :, t0:t0 + T], ebf[e][:, :T], [f"ebf{e}"], [f"qT{ti}"])
                        elif grp == 1:
                            k.act(e32[e][:, :T], pp[:, :T], AF.Sigmoid, [f"ps{s}"], [f"e32{e}"], scale=-1.0)
                            k.ts("dve", e32[e][:, :T], e32[e][:, :T], oml_fm[:, h:h + 1], None, ALU.mult, None,
                                 [f"e32{e}", "omlfm"], [f"e32{e}"])
                            if not own:
                                k.tt("dve", e32[e][:, :T], e32[e][:, :T], vmf[ti % 2][:, :T], ALU.mult,
                                     [f"e32{e}", f"vmf{ti % 2}"], [f"e32{e}"])
                            k.dma("pool", omfT[h, :, t0:t0 + T], e32[e][:, :T], [f"e32{e}"], [f"omfT{ti}"])
                        elif grp == 2:
                            k.act(ebf[e][:, :T], pp[:, :T], AF.Silu, [f"ps{s}"], [f"ebf{e}"])
                            k.dma("pool", ogT[h, :, t0:t0 + T], ebf[e][:, :T], [f"ebf{e}"], [f"ogT{ti}"])
                        elif grp == 3:
                            k.act(ebf[e][:, :T], pp[:, :T], AF.Copy, [f"ps{s}"], [f"ebf{e}"], scale=qs)
                            k.dma("pool", sqT[h, :, t0:t0 + T], ebf[e][:, :T], [f"ebf{e}"], [f"sqT{ti}"])
                        elif grp == 4:
                            k.act(ebf[e][:, :T], pp[:, :T], AF.Copy, [f"ps{s}"], [f"ebf{e}"])
                            k.dma("pool", skT[h, :, t0:t0 + T], ebf[e][:, :T], [f"ebf{e}"], [f"skT{ti}"])
                        else:
                            gc = j - 40
                            k.act(ebf[e][:, :T], pp[:, :T], AF.Sigmoid, [f"ps{s}", "bg"], [f"ebf{e}"],
                                  bias=bg_s[:, gc:gc + 1])
                            k.dma("pool", gT[gc, :, t0:t0 + T], ebf[e][:, :T], [f"ebf{e}"], [f"gT{ti}"])
                    for g in range(6):
                        s = g % 2
                        fetch(wtm[s], f"wtm{s}", win_tm32[g], win_tmbf[g], f"wintm{g}", first, wst, DC * 512)
                        for tb in range(T // 128):
                            pp = ps[2 + tmc % 4]
                            pk = f"ps{2 + tmc % 4}"
                            tmc += 1
                            for c in range(DC):
                                k.mm(pp[:, :], xbf[:, c, tb * 128:(tb + 1) * 128], wtm[s][:, c * 512:(c + 1) * 512],
                                     c == 0, c == DC - 1, [f"wtm{s}", "xbf"], [pk], c == DC - 1)
                            e = ecnt % NE
                            ecnt += 1
                            row0 = t0 + tb * 128
                            cs = slice((g % 2) * 512, (g % 2) * 512 + 512)
                            if g < 2:
                                k.act(f32b[e][:], pp[:, :], AF.Sigmoid, [pk], [f"f32b{e}"])
                                k.tt("dve", f32b[e][:], f32b[e][:], oml_tm[:, cs], ALU.mult, [f"f32b{e}", "omltm"], [f"f32b{e}"])
                                k.tt("dve", f32b[e][:], f32b[e][:], lb_tm[:, cs], ALU.add, [f"f32b{e}", "lbtm"], [f"f32b{e}"])
                                k.act(lf[e][:], f32b[e][:], AF.Ln, [f"f32b{e}"], [f"lf{e}"])
                                k.ts("dve", om[e][:], f32b[e][:], -1.0, 1.0, ALU.mult, ALU.add, [f"f32b{e}"], [f"om{e}"])
                                if not own:
                                    blk = row0 // 128
                                    k.ts("dve", lf[e][:], lf[e][:], vmt[:, blk:blk + 1], None, ALU.mult, None,
                                         [f"lf{e}", "vmt"], [f"lf{e}"])
                                    k.ts("dve", om[e][:], om[e][:], vmt[:, blk:blk + 1], None, ALU.mult, None,
                                         [f"om{e}", "vmt"], [f"om{e}"])
                                k.dma("pool", logf[row0:row0 + 128, cs], lf[e][:], [f"lf{e}"], [f"logf{ti}"])
                                k.dma("pool", omf[row0:row0 + 128, cs], om[e][:], [f"om{e}"], [f"omf{ti}"])
                            else:
                                k.act(ebf[e][:], pp[:, :], AF.Copy, [pk], [f"ebf{e}"])
                                if (not own) and g >= 4:
                                    blk = row0 // 128
                                    k.ts("dve", ebf[e][:], ebf[e][:], vmt[:, blk:blk + 1], None, ALU.mult, None,
                                         [f"ebf{e}", "vmt"], [f"ebf{e}"])
                                dst = vhg if g < 4 else svv
                                k.dma("pool", dst[row0:row0 + 128, cs], ebf[e][:], [f"ebf{e}"],
                                      [("vhg" if g < 4 else "svv") + str(ti)])
                S.barrier()
                S.emit()

        def phase_b2():
            S.barrier()
            with ExitStack() as st:
                def sb(n, shp, dt):
                    return st.enter_context(nc.sbuf_tensor(f"b2_{n}", shp, dt))
                NBUF = 2
                lf2 = [sb(f"lf2{i}", [64, 2, 512], F32) for i in range(NBUF)]
                om2 = [sb(f"om2{i}", [64, 2, 512], F32) for i in range(NBUF)]
                vv2 = [sb(f"vv2{i}", [64, 2, 512], BF16) for i in range(NBUF)]
                omT = [sb(f"omT{i}", [128, 4, 128], F32) for i in range(NBUF)]
                qTt = [sb(f"qTt{i}", [128, 4, 128], BF16) for i in range(NBUF)]
                ogt = [sb(f"ogt{i}", [128, 4, 128], BF16) for i in range(NBUF)]
                esfx = [sb(f"esfx{i}", [64, 2, 512], F32) for i in range(NBUF)]
                khat = [sb(f"khat{i}", [64, 2, 512], BF16) for i in range(NBUF)]
                Ep = [sb(f"Ep{i}", [128, 512], F32) for i in range(NBUF)]
                En = [sb(f"En{i}", [128, 512], F32) for i in range(NBUF)]
                QtT = [sb(f"QtT{i}", [128, 512], BF16) for i in range(NBUF)]
                KtT = [sb(f"KtT{i}", [128, 512], BF16) for i in range(NBUF)]
                AT = [sb(f"AT{i}", [64, 512], BF16) for i in range(NBUF)]
                Sst = sb("Sst", [128, NH * 128], F32)
                Sbf = sb("Sbf", [128, NH * 128], BF16)
                osq = [sb(f"osq{i}", [128, 512], BF16) for i in range(NBUF)]
                rs = [sb(f"rs{i}", [128, 512], F32) for i in range(NBUF)]
                o1 = [sb(f"o1{i}", [128, 512], F32) for i in range(NBUF)]
                o2 = [sb(f"o2{i}", [128, 4, 128], BF16) for i in range(NBUF)]
                onesbf_t = sb("onesbf", [128, 128], BF16)
                onesbf = onesbf_t[:]
                k.ms("dve", onesbf_t[:], 1.0, ["cbf"])
                k.ms("dve", Sst[:], 0.0, ["Sst0", "Sst1"])
                k.ms("dve", Sbf[:], 0.0, ["Sbf0", "Sbf1"])
                it = 0
                for g in range(NB):
                    bi = 0 if g == 0 else (g - 1) // 4 + 1
                    for hg in range(2):
                        b = it % NBUF
                        it += 1
                        rows = slice(g * 128, (g + 1) * 128)
                        cols = slice(hg * 512, (hg + 1) * 512)
                        hs = slice(hg * 4, hg * 4 + 4)
                        k.dma("sp", lf2[b][:], logf[rows, cols].rearrange("(c s) k -> s c k", c=2), [f"logf{bi}"], [f"lf2{b}"])
                        k.dma("sp", om2[b][:], omf[rows, cols].rearrange("(c s) k -> s c k", c=2), [f"omf{bi}"], [f"om2{b}"])
                        k.dma("sp", vv2[b][:], vhg[rows, cols].rearrange("(c s) k -> s c k", c=2), [f"vhg{bi}"], [f"vv2{b}"])
                        own = g >= OWNB
                        if own:
                            k.dma("sp", omT[b][:], omfT[hs, :, rows].rearrange("h p t -> p h t"), [f"omfT{bi}"], [f"omT{b}"])
                            k.dma("sp", qTt[b][:], qT[hs, :, rows].rearrange("h p t -> p h t"), [f"qT{bi}"], [f"qTt{b}"])
                            k.dma("sp", ogt[b][:], ogT[hs, :, rows].rearrange("h p t -> p h t"), [f"ogT{bi}"], [f"ogt{b}"])
                        for c in range(2):
                            k.mm(ps[c][0:64, :], SU64, lf2[b][:, c, :], True, True, ["c32", f"lf2{b}"], [f"ps{c}"], True)
                        for h in range(4):
                            for c in range(2):
                                o0 = (h * 2 + c) * 64
                                k.mm(ps[2][:, o0:o0 + 64], lf2[b][:, c, h * 128:(h + 1) * 128], TRI64, True, True,
                                     ["c32", f"lf2{b}"], ["ps2"], h == 3 and c == 1)
                        for c in range(2):
                            k.act(esfx[b][:, c, :], ps[c][0:64, :], AF.Exp, [f"ps{c}"], [f"esfx{b}"])
                        k.tt("dve", khat[b][:], om2[b][:], esfx[b][:], ALU.mult, [f"om2{b}", f"esfx{b}"], [f"khat{b}"])
                        k.act(Ep[b][:], ps[2][:, :], AF.Exp, ["ps2"], [f"Ep{b}"])
                        if own:
                            k.act(En[b][:], ps[2][:, :], AF.Exp, ["ps2"], [f"En{b}"], scale=-1.0)
                            k.tt("dve", QtT[b][:], qTt[b][:].rearrange("p h t -> p (h t)"), Ep[b][:], ALU.mult,
                                 [f"qTt{b}", f"Ep{b}"], [f"QtT{b}"])
                            k.tt("dve", KtT[b][:], omT[b][:].rearrange("p h t -> p (h t)"), En[b][:], ALU.mult,
                                 [f"omT{b}", f"En{b}"], [f"KtT{b}"])
                            for h in range(4):
                                for c in range(2):
                                    o0 = (h * 2 + c) * 64
                                    k.mm(ps[3][0:64, o0:o0 + 64], KtT[b][:, o0:o0 + 64], QtT[b][:, o0:o0 + 64], True, True,
                                         [f"KtT{b}", f"QtT{b}"], ["ps3"], h == 3 and c == 1)
                            k.tt("dve", AT[b][:], ps[3][0:64, :], MASKLE, ALU.mult, ["ps3", "c32"], [f"AT{b}"])
                        sk = f"Sst{hg}"
                        sbk = f"Sbf{hg}"
                        for c in range(2):
                            for h in range(4 if own else 0):
                                o0 = (h * 2 + c) * 64
                                hh = hg * 4 + h
                                k.mm(ps[4][:, o0:o0 + 64], vv2[b][:, c, h * 128:(h + 1) * 128], AT[b][:, o0:o0 + 64],
                                     True, False, [f"vv2{b}", f"AT{b}"], ["ps4"], False)
                                k.mm(ps[4][:, o0:o0 + 64], Sbf[:, hh * 128:(hh + 1) * 128], QtT[b][:, o0:o0 + 64],
                                     False, True, [sbk, f"QtT{b}"], ["ps4"], h == 3)
                            for h in range(4):
                                k.mm(ps[5][:, h * 128:(h + 1) * 128], khat[b][:, c, h * 128:(h + 1) * 128],
                                     vv2[b][:, c, h * 128:(h + 1) * 128], True, True, [f"khat{b}", f"vv2{b}"], ["ps5"], h == 3)
                            for h in range(4):
                                hh = hg * 4 + h
                                col = (h * 2 + c) * 64 + 63
                                k.stt("dve", Sst[:, hh * 128:(hh + 1) * 128], Sst[:, hh * 128:(hh + 1) * 128],
                                      Ep[b][:, col:col + 1], ps[5][:, h * 128:(h + 1) * 128], ALU.mult, ALU.add,
                                      [sk, f"Ep{b}", "ps5"], [sk])
                            if g >= OWNB - 1:
                                k.cp("act", Sbf[:, hg * 512:(hg + 1) * 512], Sst[:, hg * 512:(hg + 1) * 512], [sk], [sbk])
                        if not own:
                            continue
                        k.act(osq[b][:], ps[4][:, :], AF.Square, ["ps4"], [f"osq{b}"])
                        k.mm(ps[6][:, :], onesbf, osq[b][:], True, True, ["cbf", f"osq{b}"], ["ps6"], True)
                        k.act(rs[b][:], ps[6][:, :], AF.Sqrt, ["ps6"], [f"rs{b}"], bias=RMS_EPS, scale=1.0 / 128.0)
                        k.recip(rs[b][:], rs[b][:], [f"rs{b}"], [f"rs{b}"])
                        k.tt("dve", o1[b][:], ps[4][:, :], rs[b][:], ALU.mult, ["ps4", f"rs{b}"], [f"o1{b}"])
                        for h in range(4):
                            hh = hg * 4 + h
                            k.stt("dve", o2[b][:, h, :], o1[b][:, h * 128:(h + 1) * 128], gn_s[:, hh:hh + 1], ogt[b][:, h, :],
                                  ALU.mult, ALU.mult, [f"o1{b}", "gn", f"ogt{b}"], [f"o2{b}"])
                        k.dma("pool", ohgT[hs, :, rows].rearrange("h p t -> p h t"), o2[b][:], [f"o2{b}"], [f"ohg{bi}"])
                S.barrier()
                S.emit()

        def phase_b3():
            S.barrier()
            with ExitStack() as st:
                def sb(n, shp, dt):
                    return st.enter_context(nc.sbuf_tensor(f"b3_{n}", shp, dt))
                KT = [sb(f"KT{i}", [128, L], BF16) for i in range(2)]
                VV = [sb(f"VV{i}", [128, NB, 128], BF16) for i in range(2)]
                QT = [sb(f"QT{i}", [128, 512], BF16) for i in range(2)]
                ee = [sb(f"ee{i}", [128, 512], F32) for i in range(2)]
                sp = [sb(f"sp{i}", [128, 512], BF16) for i in range(3)]
                accb = [sb(f"accb{i}", [128, 512], BF16) for i in range(2)]
                ww = [sb(f"ww{i}", [128, 512], BF16) for i in range(3)]
                oo = [sb(f"oo{i}", [128, 512], BF16) for i in range(2)]
                cBs = sb("cBs", [128, 4352], F32)
                cbf = sb("cbf", [128, 4352], BF16)
                nones_t = sb("nones", [128, 128], BF16)
                k.dma("sp", cBs[:], constsB, [], ["cBs"])
                k.cp("dve", cbf[:], cBs[:], ["cBs"], ["cbf"])
                k.ms("dve", nones_t[:], -1.0, ["cbf"])
                NUI = cbf[:, 0:128]
                IDENT = cbf[:, 128:256]
                NONES = nones_t[:]
                MJ = [cbf[:, 256 + 512 * j:768 + 512 * j] for j in range(4)]
                NEGM = [cbf[:, 2304 + 512 * j:2816 + 512 * j] for j in range(4)]
                allk = [f"skT{i}" for i in range(len(tilesA))]
                allv = [f"svv{i}" for i in range(len(tilesA))]
                def head_loads(h):
                    hb = h % 2
                    for b0 in range(0, NB, 8):
                        b1 = min(NB, b0 + 8)
                        k.dma("sp", VV[hb][:, b0:b1, :],
                              svv[b0 * 128:b1 * 128, h * 128:(h + 1) * 128].rearrange("(b s) d -> s b d", s=128),
                              allv, [f"VV{hb}"])
                    for c0 in range(0, L, 2048):
                        c1 = min(L, c0 + 2048)
                        k.dma("sp", KT[hb][:, c0:c1], skT[h, :, c0:c1], allk, [f"KT{hb}"])

                tl = []
                gcnt = 0
                for h in range(NH):
                    for qg in range(NTO, NT):
                        fb = 1 + 4 * qg
                        kbs = list(range(fb + 3, -1, -1))
                        gcnt += 1
                        for ki, kb in enumerate(kbs):
                            tl.append(dict(h=h, hb=h % 2, qg=qg, gp=gcnt % 2, ki=ki, kb=kb, n=len(kbs),
                                           dj=kb - fb, idx=len(tl)))

                def s1(t):
                    h, hb, qg, gp, ki, kb, idx = t["h"], t["hb"], t["qg"], t["gp"], t["ki"], t["kb"], t["idx"]
                    if ki == 0:
                        t0 = 128 + qg * 512
                        k.dma("sp", QT[gp][:, :], sqT[h, :, t0:t0 + 512], [f"sqT{1 + qg}"], [f"QT{gp}"])
                    b2, b3 = idx % 2, idx % 3
                    Kblk = KT[hb][:, kb * 128:(kb + 1) * 128]
                    k.mm(ps[b2][:, :], Kblk, QT[gp][:, :], True, True, [f"KT{hb}", f"QT{gp}"], [f"ps{b2}"], True)
                    k.act(ee[b2][:], ps[b2][:, :], AF.Exp, [f"ps{b2}"], [f"ee{b2}"])

                def s1b(t):
                    idx = t["idx"]
                    b2, b3 = idx % 2, idx % 3
                    k.act(sp[b3][:], ee[b2][:], AF.Ln, [f"ee{b2}"], [f"sp{b3}"], bias=1.0)
                    if t["dj"] >= 0:
                        k.tt("dve", sp[b3][:], sp[b3][:], MJ[t["dj"]], ALU.mult, [f"sp{b3}", "cbf"], [f"sp{b3}"])

                def s2(t):
                    h, hb, qg, gp, ki, kb, idx, n, dj = (t["h"], t["hb"], t["qg"], t["gp"], t["ki"], t["kb"], t["idx"],
                                                          t["n"], t["dj"])
                    b2, b3 = idx % 2, idx % 3
                    Kblk = KT[hb][:, kb * 128:(kb + 1) * 128]
                    cps, cpk = ps[4 + gp], f"ps{4 + gp}"
                    if ki < n - 1:
                        k.mm(cps[:, :], NONES, sp[b3][:], ki == 0, ki == n - 2, ["cbf", f"sp{b3}"], [cpk], True)
                        k.cp("dve", accb[b2][:], cps[:, :], [cpk], [f"accb{b2}"])
                    pl, plk = ps[2 + b2], f"ps{2 + b2}"
                    k.mm(pl[:, :], Kblk, QT[gp][:, :], True, False, [f"KT{hb}", f"QT{gp}"], [plk], False)
                    if ki > 0:
                        k.mm(pl[:, :], IDENT, accb[(idx - 1) % 2][:], False, False, ["cbf", f"accb{(idx - 1) % 2}"], [plk], False)
                    if dj >= 0:
                        k.mm(pl[:, :], IDENT, NEGM[dj], False, False, ["cbf"], [plk], False)
                    k.mm(pl[:, :], NUI, sp[b3][:], False, True, ["cbf", f"sp{b3}"], [plk], True)
                    k.act(ww[b3][:], pl[:, :], AF.Exp, [plk], [f"ww{b3}"])

                def s3(t):
                    h, hb, qg, gp, ki, kb, idx, n = t["h"], t["hb"], t["qg"], t["gp"], t["ki"], t["kb"], t["idx"], t["n"]
                    b3 = idx % 3
                    po, pok = ps[6 + gp], f"ps{6 + gp}"
                    k.mm(po[:, :], VV[hb][:, kb, :], ww[b3][:], ki == 0, ki == n - 1, [f"VV{hb}", f"ww{b3}"], [pok], True)
                    if ki == n - 1:
                        t0 = 128 + qg * 512
                        k.act(oo[gp][:], po[:, :], AF.Copy, [pok], [f"oo{gp}"])
                        k.dma("pool", osbT[h, :, t0:t0 + 512], oo[gp][:], [f"oo{gp}"], [f"osb{1 + qg}"])

                ntl = len(tl)
                head_loads(0)
                if NH > 1:
                    head_loads(1)
                bg = []
                wstb = [sb(f"wstb{i}", [128, 2048], F32) for i in range(2)]
                cvb = [sb(f"cvb{i}", [128, FC * 128], BF16) for i in range(2)]

                def bg_slab(src32, scr, skey, n):
                    def f():
                        if skey in converted:
                            return
                        converted.add(skey)
                        z = bg_slab.ctr % 2
                        bg_slab.ctr += 1
                        for p0 in range(0, n, 2048):
                            m = min(2048, n - p0)
                            j = bg_slab.pc % 2
                            bg_slab.pc += 1
                            k.dma("sp", wstb[j][:, :m], src32[:, p0:p0 + m], [], [f"wstb{j}"])
                            k.cp("pool", cvb[z][:, p0:p0 + m], wstb[j][:, :m], [f"wstb{j}"], [f"cvb{z}"])
                        k.dma("pool", scr, cvb[z][:, :n], [f"cvb{z}"], [skey])
                    return f
                bg_slab.ctr = 0
                bg_slab.pc = 0
                for c in range(DC):
                    bg.append(bg_slab(phg32[c], phgbf[c], f"phg{c}", 8 * 128))
                    bg.append(bg_slab(psb32[c], psbbf[c], f"psb{c}", 8 * 128))
                for c in range(DC):
                    bg.append(bg_slab(wout32[c], woutbf[c], f"wout{c}", DC * 128))
                W2 = Wi["f2"]
                for f_ in range(FC):
                    bg.append(bg_slab(W2["g32"][f_], W2["gbf"][f_], f"fcg{f_}", DC * 128))
                    bg.append(bg_slab(W2["u32"][f_], W2["ubf"][f_], f"fcu{f_}", DC * 128))
                for c in range(DC):
                    bg.append(bg_slab(W2["d32"][c], W2["dbf"][c], f"fcd{c}", FC * 128))
                bg_every = max(1, (ntl - 8) // (len(bg) + 1))
                for i in range(ntl + 2):
                    if i < ntl:
                        s1(tl[i])
                    if 0 <= i - 1 < ntl:
                        s2(tl[i - 1])
                    if 0 <= i - 2 < ntl:
                        s3(tl[i - 2])
                        t = tl[i - 2]
                        if t["ki"] == 0 and t["qg"] == NTO and 1 <= t["h"] and t["h"] + 1 < NH:
                            head_loads(t["h"] + 1)
                    if i < ntl:
                        s1b(tl[i])
                    if bg and i % bg_every == bg_every - 1:
                        bg.pop(0)()
                while bg:
                    bg.pop(0)()
                S.barrier()
                S.emit()

        def phase_c1():
            S.barrier()
            with ExitStack() as st:
                def sb(n, shp, dt):
                    return st.enter_context(nc.sbuf_tensor(f"c1_{n}", shp, dt))
                ahg = sb("ahg", [128, 8, 512], BF16)
                asb = sb("asb", [128, 8, 512], BF16)
                gt = sb("gt", [128, 32, 512], BF16)
                yT = sb("yT", [128, DC, 512], BF16)
                wst = [sb(f"wst{i}", [128, 2048], F32) for i in range(3)]
                wp1 = [sb(f"wp1{i}", [128, 8 * 128], BF16) for i in range(2)]
                wp2 = [sb(f"wp2{i}", [128, 8 * 128], BF16) for i in range(2)]
                wo = [sb(f"wo{i}", [128, DC * 128], BF16) for i in range(2)]
                ta = [sb(f"ta{i}", [128, 512], F32) for i in range(2)]
                tb_ = [sb(f"tb{i}", [128, 512], F32) for i in range(2)]
                xs = [sb(f"xs{i}", [128, 512], F32) for i in range(2)]
                r = sb("r", [128, DC, 512], F32)
                sqb = [sb(f"sqb{i}", [128, 512], BF16) for i in range(2)]
                rhi = [sb(f"rhi{i}", [128, 512], BF16) for i in range(2)]
                rlo = [sb(f"rlo{i}", [128, 512], BF16) for i in range(2)]
                mean = sb("mean", [128, 512], F32)
                msq = sb("msq", [128, 512], F32)
                rstd = sb("rstd", [128, 512], F32)
                t1 = [sb(f"t1{i}", [128, 512], F32) for i in range(2)]
                yo = [sb(f"yo{i}", [128, 512], F32) for i in range(2)]
                yb = [sb(f"yb{i}", [128, 512], BF16) for i in range(2)]
                T = 512
                pending = []
                for ti, (o0, _) in enumerate(tilesC):
                    first = ti == 0
                    t0 = OWN0 + o0
                    ai = OWNT + ti
                    k.dma("sp", ahg[:], ohgT[:, :, t0:t0 + T].rearrange("h p t -> p h t"), [f"ohg{ai}"], ["ahg"])
                    k.dma("sp", asb[:], osbT[:, :, t0:t0 + T].rearrange("h p t -> p h t"), [f"osb{ai}"], ["asb"])
                    for c0 in range(0, 32, 8):
                        k.dma("sp", gt[:, c0:c0 + 8, :], gT[c0:c0 + 8, :, t0:t0 + T].rearrange("c p t -> p c t"), [f"gT{ai}"], ["gt"])
                    for c in range(DC):
                        s = c % 2
                        if pending:
                            pending.pop(0)()
                        fetch(wp1[s], f"wp1{s}", phg32[c], phgbf[c], f"phg{c}", first, wst, 8 * 128)
                        fetch(wp2[s], f"wp2{s}", psb32[c], psbbf[c], f"psb{c}", first, wst, 8 * 128)
                        for kc in range(8):
                            k.mm(ps[s][:, :], wp1[s][:, kc * 128:(kc + 1) * 128], ahg[:, kc, :], kc == 0, kc == 7,
                                 [f"wp1{s}", "ahg"], [f"ps{s}"], kc == 7)
                        for kc in range(8):
                            k.mm(ps[2 + s][:, :], wp2[s][:, kc * 128:(kc + 1) * 128], asb[:, kc, :], kc == 0, kc == 7,
                                 [f"wp2{s}", "asb"], [f"ps{2 + s}"], kc == 7)
                        k.tt("dve", ta[s][:], ps[s][:, :], gt[:, c, :], ALU.mult, [f"ps{s}", "gt"], [f"ta{s}"])
                        k.tt("dve", tb_[s][:], ps[2 + s][:, :], gt[:, 16 + c, :], ALU.mult, [f"ps{2 + s}", "gt"], [f"tb{s}"])
                        k.tt("dve", yT[:, c, :], ta[s][:], tb_[s][:], ALU.add, [f"ta{s}", f"tb{s}"], [f"yT{c}"])
                    while pending:
                        pending.pop(0)()

                    def stats(d):
                        z = d % 2
                        k.mm(ps[6][:, :], onesB, rhi[z][:], d == 0, False, [f"rhi{z}", "onesb"], ["ps6"], False)
                        k.mm(ps[6][:, :], onesB, rlo[z][:], False, d == DC - 1, [f"rlo{z}", "onesb"], ["ps6"], True)
                        k.mm(ps[7][:, :], onesB, sqb[z][:], d == 0, d == DC - 1, [f"sqb{z}", "onesb"], ["ps7"], True)

                    for c in range(DC):
                        s = c % 2
                        fetch(wo[s], f"wo{s}", wout32[c], woutbf[c], f"wout{c}", first, wst, DC * 128)
                        pm = ps[4 + s]
                        for kc in range(DC):
                            k.mm(pm[:, :], wo[s][:, kc * 128:(kc + 1) * 128], yT[:, kc, :], kc == 0, kc == DC - 1,
                                 [f"wo{s}", f"yT{kc}"], [f"ps{4 + s}"], kc == DC - 1)
                        if c > 0:
                            stats(c - 1)
                        k.dma("sp", xs[s][:], h1T[c, :, t0:t0 + T], [f"h1_{ai}_{c}_0"], [f"xs{s}"])
                        k.stt("dve", r[:, c, :], xs[s][:], ALPHA, pm[:, :], ALU.mult, ALU.add, [f"xs{s}", f"ps{4 + s}"], [f"r{c}"])
                        k.act(sqb[s][:], r[:, c, :], AF.Square, [f"r{c}"], [f"sqb{s}"])
                        k.cp("dve", rhi[s][:], r[:, c, :], [f"r{c}"], [f"rhi{s}"])
                        k.tt("dve", rlo[s][:], r[:, c, :], rhi[s][:], ALU.subtract, [f"r{c}", f"rhi{s}"], [f"rlo{s}"])
                    stats(DC - 1)
                    pending = ln_finish(r, T, LN_EPS, lnp_s[:, 2 * DC:3 * DC], lnp_s[:, 3 * DC:4 * DC], mean, msq, rstd, t1, yo, yb,
                                        lambda c, o0=o0: h2T[c, :, o0:o0 + T], lambda c, o0=o0: h2Tbf[c, :, o0:o0 + T],
                                        lambda c, b, ti=ti: f"h2_{ti}_{c}_{b}")
                while pending:
                    pending.pop(0)()
                S.barrier()
                S.emit()

        ffn_phase("fa", tilesA, xT, True, lambda ti, c, b: f"xin", xT, Wi["f1"], h1T, h1Tbf,
                  lambda ti, c, b: f"h1_{ti}_{c}_{b}", lnp_s[:, 0:DC], lnp_s[:, DC:2 * DC])
        phase_b1()
        phase_b2()
        phase_b3()
        phase_c1()
        ffn_phase("fc", tilesC, h2Tbf, False, lambda ti, c, b: f"h2_{ti}_{c}_{b}", h2T, Wi["f2"], outT, None,
                  lambda ti, c, b: f"out_{ti}_{c}", lnp_s[:, 4 * DC:5 * DC], lnp_s[:, 5 * DC:6 * DC])
        S.final_wait()
        S.emit()
    return nc


def _fm(W, kc, oc):
    return np.ascontiguousarray(W.reshape(kc, 128, oc, 128).transpose(2, 1, 0, 3)).reshape(oc, 128, kc * 128)


def _tm(W, kc, g):
    return np.ascontiguousarray(W.reshape(kc, 128, g, 512).transpose(2, 1, 0, 3)).reshape(g, 128, kc * 512)


def _consts():
    cA = np.zeros((128, 768), np.float32)
    cA[:, 0:128] = 1.0
    i = np.arange(64)
    cA[0:64, 128:192] = (i[:, None] > i[None, :])
    cA[0:64, 192:256] = (i[:, None] <= i[None, :])
    cA[0:64, 256:768] = np.tile((i[:, None] <= i[None, :]).astype(np.float32), (1, 8))
    cB = np.zeros((128, 4352), np.float32)
    p = np.arange(128)
    cB[:, 0:128] = -(p[:, None] >= p[None, :]).astype(np.float32)
    cB[:, 128:256] = np.eye(128, dtype=np.float32)
    t = np.arange(512)
    for j in range(4):
        m = ((j * 128 + p[:, None]) < t[None, :]).astype(np.float32)
        cB[:, 256 + 512 * j:768 + 512 * j] = m
        cB[:, 2304 + 512 * j:2816 + 512 * j] = -30000.0 * (1.0 - m)
    return cA, cB


def make_in_maps(inp, SEQ):
    f = lambda a: np.asarray(a, np.float32)
    x = f(inp["x"])
    B = x.shape[0]
    meta = f(inp["meta"])
    vec = lambda v: np.ascontiguousarray(f(v).reshape(-1, 128).T)
    lnp = np.concatenate([vec(inp[n][0]) for n in ("ln1_g", "ln1_b", "ln2_g", "ln2_b", "ln3_g", "ln3_b")], axis=1)
    w_in = f(inp["w_in"][0])
    fm_cols = np.concatenate([np.arange(0, 1024), np.arange(1024, 2048), np.arange(3072, 4096),
                              np.arange(4096, 5120), np.arange(5120, 6144), np.arange(7168, 11264)])
    tm_cols = np.concatenate([np.arange(1024, 2048), np.arange(2048, 3072), np.arange(6144, 7168)])
    lbl = f(inp["hg_lb_logits"])
    cA, cB = _consts()
    shared = {
        "constsA": cA, "constsB": cB,
        "lnp": np.ascontiguousarray(lnp),
        "bgate": vec(inp["b_gate"][0]),
        "gnorm": vec(inp["hg_norm_g"][0]),
        "lbl_fm": np.ascontiguousarray(np.concatenate([vec(lbl[0]), vec(lbl[1])], axis=1)),
        "lbl_tm": np.ascontiguousarray(np.broadcast_to(np.concatenate([lbl[0], lbl[1]])[None, :], (128, 2048))),
        "f1_g": _fm(f(inp["ffn1_w_gate"][0]), DC, FC), "f1_u": _fm(f(inp["ffn1_w_up"][0]), DC, FC),
        "f1_d": _fm(f(inp["ffn1_w_down"][0]), FC, DC),
        "f2_g": _fm(f(inp["ffn2_w_gate"][0]), DC, FC), "f2_u": _fm(f(inp["ffn2_w_up"][0]), DC, FC),
        "f2_d": _fm(f(inp["ffn2_w_down"][0]), FC, DC),
        "win_fm": _fm(np.ascontiguousarray(w_in[:, fm_cols]), DC, 72),
        "win_tm": _tm(np.ascontiguousarray(w_in[:, tm_cols]), DC, 6),
        "phg": _fm(f(inp["w_proj_hg"][0]), 8, DC), "psb": _fm(f(inp["w_proj_sb"][0]), 8, DC),
        "wout": _fm(f(inp["w_out"][0]), DC, DC),
    }
    maps = []
    L = SEQ + 128
    SO = SEQ // 2
    p = np.arange(128)
    for b in range(B):
        for j in range(2):
            xf = np.zeros((L, D), np.float32)
            if j == 1:
                xf[PADN:128] = meta
                xf[128:] = x[b]
                v0 = PADN
            else:
                xf[SO + PADN:SO + 128] = meta
                xf[SO + 128:] = x[b][:SO]
                v0 = SO + PADN
            valid = (np.arange(L) >= v0).astype(np.float32)
            m = dict(shared)
            m["xT"] = np.ascontiguousarray(xf.T).reshape(DC, 128, L)
            m["vm_tm"] = np.ascontiguousarray(valid.reshape(L // 128, 128).T)
            m["vm_fm"] = np.ascontiguousarray(np.broadcast_to(valid[None, :], (128, L)))
            maps.append(m)
    return maps


_CACHE = {}


def kernel(**inputs):
    x = np.asarray(inputs["x"])
    B, SEQ, _ = x.shape
    if SEQ not in _CACHE:
        _CACHE[SEQ] = build_program(SEQ)
    nc = _CACHE[SEQ]
    maps = make_in_maps(inputs, SEQ)
    res = run_bass_kernel_spmd(nc, maps, core_ids=list(range(2 * B)))
    out = np.empty((B, SEQ, D), np.float32)
    SO = SEQ // 2
    for b in range(B):
        for j in range(2):
            out[b, j * SO:(j + 1) * SO] = res.results[2 * b + j]["outT"].reshape(D, SO).T
    return out
```

```python
import math
from contextlib import ExitStack

import numpy as np
import concourse.bass as bass
import concourse.mybir as mybir
from concourse.bass_utils import run_bass_kernel_spmd

F32 = mybir.dt.float32
BF16 = mybir.dt.bfloat16
AF = mybir.ActivationFunctionType
ALU = mybir.AluOpType

D = 2048
DC = 16
FF = 5632
FC = 44
NH = 8
ALPHA = 2.0 ** 0.25
LN_EPS = 1e-5
RMS_EPS = 1e-6
NMETA = 16
PADN = 112
ENGS = ["pe", "act", "dve", "pool", "sp"]
NDS = 32
CH = 16000
NCHUNK = 8


class Sched:
    def __init__(self, nc, st):
        self.nc = nc
        self.ops = {e: [] for e in ENGS}
        self.cnt = {e: 0 for e in ENGS}
        self.res_w = {}
        self.res_r = {}
        self.waited = {e: {} for e in ENGS}
        self.ndma = 0
        self.ndma_q = [0, 0]
        self.dma_last = [0] * NDS
        self.esem = {e: [st.enter_context(nc.semaphore(f"s_{e}{i}")) for i in range(NCHUNK)] for e in ENGS[:4]}
        self.dsem = [st.enter_context(nc.semaphore(f"s_d{i}")) for i in range(NDS)]
        self.nops = 0

    def _need(self, eng, tok, waits):
        if tok[0] == "e":
            if tok[1] == eng and eng == "pe":
                return
            key = ("e", tok[1])
        else:
            key = ("d", tok[1])
        if self.waited[eng].get(key, 0) >= tok[2]:
            return
        self.waited[eng][key] = tok[2]
        waits.append(tok)

    def op(self, eng, fn, r=(), w=(), sig=True, dma=False):
        deps = []
        for k in r:
            t = self.res_w.get(k)
            if t is not None:
                deps.append(t)
        for k in w:
            t = self.res_w.get(k)
            if t is not None:
                deps.append(t)
            rr = self.res_r.get(k)
            if rr:
                for kk, v in rr.items():
                    if kk == "dma":
                        deps.extend(v)
                    else:
                        deps.append(v)
        waits = []
        for t in deps:
            self._need(eng, t, waits)
        if dma:
            half = NDS // 2
            qi = 0 if eng == "sp" else 1
            n = self.ndma_q[qi]
            self.ndma_q[qi] += 1
            k = qi * half + n % half
            if self.dma_last[k] > 0:
                self._need(eng, ("d", k, self.dma_last[k]), waits)
            v = self.dma_last[k] + 16
            self.ndma += 1
            self.dma_last[k] = v
            tok = ("d", k, v)
        else:
            if sig:
                self.cnt[eng] += 1
                tok = ("e", eng, self.cnt[eng])
            else:
                tok = ("e", eng, self.cnt[eng] + 1)
        for k in w:
            self.res_w[k] = tok
            self.res_r[k] = {}
        for k in r:
            d = self.res_r.setdefault(k, {})
            if dma:
                d.setdefault("dma", []).append(tok)
            else:
                d[eng] = tok
        self.ops[eng].append((waits, fn, tok if (sig or dma) else None))
        self.nops += 1

    def barrier(self):
        for e in ENGS:
            waits = []
            for e2 in ENGS[:4]:
                if e2 != e and self.cnt[e2] > 0:
                    self._need(e, ("e", e2, self.cnt[e2]), waits)
            for k in range(NDS):
                if self.dma_last[k] > 0:
                    self._need(e, ("d", k, self.dma_last[k]), waits)
            if waits:
                self.ops[e].append((waits, None, None))

    def _semval(self, tok):
        if tok[0] == "e":
            c = (tok[2] - 1) // CH
            assert c < NCHUNK, "semaphore chunks exhausted"
            return self.esem[tok[1]][c], (tok[2] - 1) % CH + 1
        return self.dsem[tok[1]], tok[2]

    def emit(self):
        nc = self.nc
        for e in ENGS[:4]:
            for (waits, fn, tok) in self.ops[e]:
                for t in waits:
                    if t[0] == "e":
                        assert t[2] <= self.cnt[t[1]], ("unsignaled dependency", e, t)
        with nc.Block() as block:
            def runner(name):
                def f(eng):
                    for (waits, fn, tok) in self.ops[name]:
                        for t in waits:
                            s, v = self._semval(t)
                            eng.wait_ge(s, v)
                        if fn is None:
                            continue
                        ins = fn(eng)
                        if tok is not None:
                            if tok[0] == "d":
                                ins.then_inc(self.dsem[tok[1]], 16)
                            else:
                                s, _ = self._semval(tok)
                                ins.then_inc(s, 1)
                return f
            block.sync(runner("sp"))
            block.tensor(runner("pe"))
            block.scalar(runner("act"))
            block.vector(runner("dve"))
            block.gpsimd(runner("pool"))
        for e in ENGS:
            self.ops[e] = []

    def final_wait(self):
        waits = []
        for k in range(NDS):
            if self.dma_last[k] > 0:
                self._need("sp", ("d", k, self.dma_last[k]), waits)
        for e2 in ENGS[:4]:
            if self.cnt[e2] > 0:
                self._need("sp", ("e", e2, self.cnt[e2]), waits)
        if waits:
            self.ops["sp"].append((waits, None, None))


class K:
    def __init__(self, S):
        self.S = S

    def mm(self, out, lhsT, rhs, start, stop, r, w, sig):
        self.S.op("pe", lambda e: e.matmul(out, lhsT, rhs, start=start, stop=stop), r, w, sig=sig)

    def act(self, out, in_, func, r, w, bias=None, scale=None):
        kw = {}
        if bias is not None:
            kw["bias"] = bias
        if scale is not None:
            kw["scale"] = scale
        self.S.op("act", lambda e: e.activation(out, in_, func, **kw), r, w)

    def tt(self, eng, out, in0, in1, op, r, w):
        self.S.op(eng, lambda e: e.tensor_tensor(out, in0, in1, op), r, w)

    def ts(self, eng, out, in0, s1, s2, op0, op1, r, w):
        if op1 is None:
            self.S.op(eng, lambda e: e.tensor_scalar(out, in0, s1, None, op0), r, w)
        else:
            self.S.op(eng, lambda e: e.tensor_scalar(out, in0, s1, s2, op0, op1), r, w)

    def stt(self, eng, out, in0, scalar, in1, op0, op1, r, w):
        self.S.op(eng, lambda e: e.scalar_tensor_tensor(out, in0, scalar, in1, op0, op1), r, w)

    def recip(self, out, in_, r, w):
        self.S.op("dve", lambda e: e.reciprocal(out, in_), r, w)

    def cp(self, eng, out, in_, r, w):
        if eng == "act":
            self.S.op(eng, lambda e: e.copy(out, in_), r, w)
        else:
            self.S.op(eng, lambda e: e.tensor_copy(out, in_), r, w)

    def ms(self, eng, ap, val, w):
        self.S.op(eng, lambda e: e.memset(ap, val), (), w)

    def dma(self, q, out, in_, r, w):
        self.S.op(q, lambda e: e.dma_start(out=out, in_=in_), r, w, dma=True)


def cdiv(a, b):
    return (a + b - 1) // b


def build_program(SEQ, debug=False):
    L = SEQ + 128
    NB = L // 128
    assert SEQ % 512 == 0
    NT = SEQ // 512
    assert NT % 2 == 0
    NTO = NT // 2
    OWN0 = 128 + 512 * NTO
    OWNB = OWN0 // 128
    OWNT = 1 + NTO
    SO = SEQ // 2
    tilesA = [(0, 128)] + [(128 + 512 * i, 512) for i in range(NT)]
    tilesC = [(512 * i, 512) for i in range(NTO)]

    nc = bass.Bass("TRN2", target_bir_lowering=False)

    def din(name, shape, dt=F32):
        return nc.dram_tensor(name, list(shape), dt, kind="ExternalInput").ap()

    def dscr(name, shape, dt):
        return nc.dram_tensor(name, list(shape), dt).ap()

    xT = din("xT", [DC, 128, L])
    constsA = din("constsA", [128, 768])
    constsB = din("constsB", [128, 4352])
    lnp = din("lnp", [128, 6 * DC])
    bgate = din("bgate", [128, 32])
    gnorm = din("gnorm", [128, NH])
    lbl_fm = din("lbl_fm", [128, 2 * NH])
    lbl_tm = din("lbl_tm", [128, 2 * 1024])
    vm_tm_d = din("vm_tm", [128, NB])
    vm_fm_d = din("vm_fm", [128, L])
    Wi = {}
    for nm in ("f1", "f2"):
        Wi[nm] = dict(g32=din(f"{nm}_g", [FC, 128, DC * 128]), u32=din(f"{nm}_u", [FC, 128, DC * 128]),
                      d32=din(f"{nm}_d", [DC, 128, FC * 128]),
                      gbf=dscr(f"{nm}_gbf", [FC, 128, DC * 128], BF16), ubf=dscr(f"{nm}_ubf", [FC, 128, DC * 128], BF16),
                      dbf=dscr(f"{nm}_dbf", [DC, 128, FC * 128], BF16))
    win_fm32 = din("win_fm", [72, 128, DC * 128])
    win_tm32 = din("win_tm", [6, 128, DC * 512])
    win_fmbf = dscr("win_fmbf", [72, 128, DC * 128], BF16)
    win_tmbf = dscr("win_tmbf", [6, 128, DC * 512], BF16)
    phg32 = din("phg", [DC, 128, 8 * 128])
    psb32 = din("psb", [DC, 128, 8 * 128])
    wout32 = din("wout", [DC, 128, DC * 128])
    phgbf = dscr("phgbf", [DC, 128, 8 * 128], BF16)
    psbbf = dscr("psbbf", [DC, 128, 8 * 128], BF16)
    woutbf = dscr("woutbf", [DC, 128, DC * 128], BF16)

    outT = nc.dram_tensor("outT", [DC, 128, SO], F32, kind="ExternalOutput").ap()

    kind_dbg = dict(kind="ExternalOutput") if debug else {}

    def dscr2(name, shape, dt):
        return nc.dram_tensor(name, list(shape), dt, **kind_dbg).ap()

    h1T = dscr2("h1T", [DC, 128, L], F32)
    h1Tbf = dscr("h1Tbf", [DC, 128, L], BF16)
    qT = dscr("qT", [NH, 128, L], BF16)
    ogT = dscr("ogT", [NH, 128, L], BF16)
    omfT = dscr("omfT", [NH, 128, L], F32)
    sqT = dscr("sqT", [NH, 128, L], BF16)
    skT = dscr("skT", [NH, 128, L], BF16)
    gT = dscr("gT", [32, 128, L], BF16)
    logf = dscr("logf", [L, 1024], F32)
    omf = dscr("omf", [L, 1024], F32)
    vhg = dscr("vhg", [L, 1024], BF16)
    svv = dscr("svv", [L, 1024], BF16)
    ohgT = dscr2("ohgT", [NH, 128, L], BF16)
    osbT = dscr2("osbT", [NH, 128, L], BF16)
    h2T = dscr2("h2T", [DC, 128, SO], F32)
    h2Tbf = dscr("h2Tbf", [DC, 128, SO], BF16)

    top = ExitStack()
    with top:
        top.enter_context(nc.allow_low_precision("bf16 matmul operands with fp32 accumulation"))
        top.enter_context(nc.allow_non_contiguous_dma("strided tile loads"))
        S = Sched(nc, top)
        k = K(S)
        ps = [top.enter_context(nc.psum_tensor(f"ps{i}", [128, 512], F32)) for i in range(8)]

        def psb_(n, shp, dt):
            return top.enter_context(nc.sbuf_tensor(n, shp, dt))

        c32 = psb_("c32", [128, 768], F32)
        lnp_s = psb_("lnp_s", [128, 6 * DC], F32)
        bg_s = psb_("bg_s", [128, 32], F32)
        gn_s = psb_("gn_s", [128, NH], F32)
        lfm = psb_("lfm", [128, 2 * NH], F32)
        oml_fm = psb_("oml_fm", [128, NH], F32)
        k.dma("sp", c32[:], constsA, [], ["c32"])
        k.dma("sp", lnp_s[:], lnp, [], ["lnp"])
        k.dma("sp", bg_s[:], bgate, [], ["bg"])
        k.dma("sp", gn_s[:], gnorm, [], ["gn"])
        k.dma("sp", lfm[:], lbl_fm, [], ["lfm"])
        ones32 = c32[:, 0:128]
        onesb_t = psb_("onesb", [128, 128], BF16)
        k.ms("dve", onesb_t[:], 1.0, ["onesb"])
        onesB = onesb_t[:]
        SU64 = c32[0:64, 128:192]
        TRI64 = c32[0:64, 192:256]
        MASKLE = c32[0:64, 256:768]
        k.tt("dve", oml_fm[:], lfm[:, NH:2 * NH], lfm[:, 0:NH], ALU.subtract, ["lfm"], ["omlfm"])
        k.act(oml_fm[:], oml_fm[:], AF.Sigmoid, ["omlfm"], ["omlfm"])

        converted = set()

        def fetch(dst, dkey, src32, scr, skey, first, wst, n):
            first = skey not in converted
            converted.add(skey)
            if first:
                ceng = fetch.rot[fetch.slab % len(fetch.rot)]
                fetch.slab += 1
                for pi, p0 in enumerate(range(0, n, 2048)):
                    m = min(2048, n - p0)
                    j = fetch.ctr % len(wst)
                    fetch.ctr += 1
                    k.dma("sp", wst[j][:, :m], src32[:, p0:p0 + m], [], [f"wst{j}"])
                    k.cp(ceng, dst[:, p0:p0 + m], wst[j][:, :m], [f"wst{j}"], [dkey])
                k.dma("pool", scr, dst[:, :n], [dkey], [skey])
            else:
                k.dma("sp", dst[:, :n], scr, [skey], [dkey])
        fetch.ctr = 0
        fetch.slab = 0
        fetch.rot = ["act", "pool", "act", "dve"]

        def ln_finish(r, T, eps_eff, g_ap, b_ap, mean, msq, rstd, t1, yo, yb, out32_fn, outbf_fn, okey_fn):
            k.act(mean[:, :T], ps[6][:, :T], AF.Copy, ["ps6"], ["mean"], scale=1.0 / D)
            k.tt("dve", msq[:, :T], mean[:, :T], mean[:, :T], ALU.mult, ["mean"], ["msq"])
            k.stt("dve", msq[:, :T], ps[7][:, :T], 1.0 / D, msq[:, :T], ALU.mult, ALU.subtract, ["ps7", "msq"], ["msq"])
            k.act(rstd[:, :T], msq[:, :T], AF.Sqrt, ["msq"], ["rstd"], bias=eps_eff)
            k.recip(rstd[:, :T], rstd[:, :T], ["rstd"], ["rstd"])
            def chunk(c):
                j = c % 2
                k.tt("dve", t1[j][:, :T], r[:, c, :T], mean[:, :T], ALU.subtract, [f"r{c}", "mean"], [f"t1{j}"])
                k.tt("dve", t1[j][:, :T], t1[j][:, :T], rstd[:, :T], ALU.mult, [f"t1{j}", "rstd"], [f"t1{j}"])
                k.ts("dve", yo[j][:, :T], t1[j][:, :T], g_ap[:, c:c + 1], b_ap[:, c:c + 1], ALU.mult, ALU.add,
                     [f"t1{j}", "lnp"], [f"yo{j}"])
                k.dma("pool", out32_fn(c), yo[j][:, :T], [f"yo{j}"], [okey_fn(c, 0)])
                if outbf_fn is not None:
                    k.cp("pool", yb[j][:, :T], yo[j][:, :T], [f"yo{j}"], [f"yb{j}"])
                    k.dma("pool", outbf_fn(c), yb[j][:, :T], [f"yb{j}"], [okey_fn(c, 1)])
            return [(lambda c=c: chunk(c)) for c in range(DC)]

        def ffn_phase(tag, tiles, x_src, x_is_f32, xkey, res_src, W, out32, outbf, okey, lg, lb_):
            S.barrier()
            with ExitStack() as st:
                def sb(n, shp, dt):
                    return st.enter_context(nc.sbuf_tensor(f"{tag}_{n}", shp, dt))
                xbfs = [sb(f"xbf{i}", [128, DC, 512], BF16) for i in range(2)]
                xp = [sb(f"xp{i}", [128, 512], F32) for i in range(2)]
                HT = sb("HT", [128, FC, 512], BF16)
                wst = [sb(f"wst{i}", [128, 2048], F32) for i in range(2)]
                wgb = [sb(f"wgb{i}", [128, DC * 128], BF16) for i in range(2)]
                wub = [sb(f"wub{i}", [128, DC * 128], BF16) for i in range(2)]
                wdb = [sb(f"wdb{i}", [128, FC * 128], BF16) for i in range(2)]
                xs = [sb(f"xs{i}", [128, 512], F32) for i in range(2)]
                sg = [sb(f"sg{i}", [128, 512], F32) for i in range(2)]
                r = sb("r", [128, DC, 512], F32)
                sqb = [sb(f"sqb{i}", [128, 512], BF16) for i in range(2)]
                rhi = [sb(f"rhi{i}", [128, 512], BF16) for i in range(2)]
                rlo = [sb(f"rlo{i}", [128, 512], BF16) for i in range(2)]
                mean = sb("mean", [128, 512], F32)
                msq = sb("msq", [128, 512], F32)
                rstd = sb("rstd", [128, 512], F32)
                t1 = [sb(f"t1{i}", [128, 512], F32) for i in range(2)]
                yo = [sb(f"yo{i}", [128, 512], F32) for i in range(2)]
                yb = [sb(f"yb{i}", [128, 512], BF16) for i in range(2)]
                def load_x(ti, oi):
                    t0, T = tiles[ti]
                    xb = xbfs[oi % 2]
                    p = oi % 2
                    if x_is_f32:
                        for c in range(DC):
                            j = c % 2
                            k.dma("sp", xp[j][:, :T], x_src[c, :, t0:t0 + T], [], [f"xp{j}"])
                            k.cp("dve", xb[:, c, :T], xp[j][:, :T], [f"xp{j}"], [f"xbf{p}_{c}"])
                    else:
                        for c0 in range(0, DC, 8):
                            k.dma("sp", xb[:, c0:c0 + 8, :T], x_src[c0:c0 + 8, :, t0:t0 + T].rearrange("c p t -> p c t"),
                                  [xkey(ti, c, 1) for c in range(c0, c0 + 8)], [f"xbf{p}_{c}" for c in range(c0, c0 + 8)])

                order = list(range(len(tiles)))
                if len(tiles) > 1 and tiles[0][1] < tiles[1][1]:
                    order[0], order[1] = 1, 0
                load_x(order[0], 0)
                pending = []
                for oi, ti in enumerate(order):
                    t0, T = tiles[ti]
                    first = oi == 0
                    xbf = xbfs[oi % 2]
                    xpar = oi % 2
                    for f in range(FC):
                        s = f % 2
                        if pending and f % 2 == 1:
                            pending.pop(0)()
                        fetch(wgb[s], f"wgb{s}", W["g32"][f], W["gbf"][f], f"{tag}g{f}", first, wst, DC * 128)
                        fetch(wub[s], f"wub{s}", W["u32"][f], W["ubf"][f], f"{tag}u{f}", first, wst, DC * 128)
                        pg, pu = ps[s], ps[2 + s]
                        for c in range(DC):
                            k.mm(pg[:, :T], wgb[s][:, c * 128:(c + 1) * 128], xbf[:, c, :T], c == 0, c == DC - 1,
                                 [f"wgb{s}", f"xbf{xpar}_{c}"], [f"ps{s}"], c == DC - 1)
                        for c in range(DC):
                            k.mm(pu[:, :T], wub[s][:, c * 128:(c + 1) * 128], xbf[:, c, :T], c == 0, c == DC - 1,
                                 [f"wub{s}", f"xbf{xpar}_{c}"], [f"ps{2 + s}"], c == DC - 1)
                        k.act(sg[s][:, :T], pg[:, :T], AF.Silu, [f"ps{s}"], [f"sg{s}"])
                        k.tt("dve", HT[:, f, :T], sg[s][:, :T], pu[:, :T], ALU.mult, [f"sg{s}", f"ps{2 + s}"], [f"HT{f}"])
                    while pending:
                        pending.pop(0)()
                    if oi + 1 < len(order):
                        load_x(order[oi + 1], oi + 1)

                    def stats(d, T=T):
                        z = d % 2
                        k.mm(ps[6][:, :T], onesB, rhi[z][:, :T], d == 0, False, [f"rhi{z}", "onesb"], ["ps6"], False)
                        k.mm(ps[6][:, :T], onesB, rlo[z][:, :T], False, d == DC - 1, [f"rlo{z}", "onesb"], ["ps6"], True)
                        k.mm(ps[7][:, :T], onesB, sqb[z][:, :T], d == 0, d == DC - 1, [f"sqb{z}", "onesb"], ["ps7"], True)

                    for dc in range(DC):
                        s = dc % 2
                        fetch(wdb[s], f"wdb{s}", W["d32"][dc], W["dbf"][dc], f"{tag}d{dc}", first, wst, FC * 128)
                        py = ps[4 + s]
                        for f in range(FC):
                            k.mm(py[:, :T], wdb[s][:, f * 128:(f + 1) * 128], HT[:, f, :T], f == 0, f == FC - 1,
                                 [f"wdb{s}", f"HT{f}"], [f"ps{4 + s}"], f == FC - 1)
                        if dc > 0:
                            stats(dc - 1)
                        k.dma("sp", xs[s][:, :T], res_src[dc, :, t0:t0 + T], [xkey(ti, dc, 0)], [f"xs{s}"])
                        k.stt("dve", r[:, dc, :T], xs[s][:, :T], 2.0 * ALPHA, py[:, :T], ALU.mult, ALU.add,
                              [f"xs{s}", f"ps{4 + s}"], [f"r{dc}"])
                        k.act(sqb[s][:, :T], r[:, dc, :T], AF.Square, [f"r{dc}"], [f"sqb{s}"])
                        k.cp("dve", rhi[s][:, :T], r[:, dc, :T], [f"r{dc}"], [f"rhi{s}"])
                        k.tt("dve", rlo[s][:, :T], r[:, dc, :T], rhi[s][:, :T], ALU.subtract, [f"r{dc}", f"rhi{s}"], [f"rlo{s}"])
                    stats(DC - 1)
                    pending = ln_finish(r, T, 4.0 * LN_EPS, lg, lb_, mean, msq, rstd, t1, yo, yb,
                                        lambda c, t0=t0, T=T: out32[c, :, t0:t0 + T],
                                        (lambda c, t0=t0, T=T: outbf[c, :, t0:t0 + T]) if outbf is not None else None,
                                        lambda c, b, ti=ti: okey(ti, c, b))
                while pending:
                    pending.pop(0)()
                S.barrier()
                S.emit()

        def phase_b1():
            S.barrier()
            with ExitStack() as st:
                def sb(n, shp, dt):
                    return st.enter_context(nc.sbuf_tensor(f"b1_{n}", shp, dt))
                xbf = sb("xbf", [128, DC, 512], BF16)
                wst = [sb(f"wst{i}", [128, 2048], F32) for i in range(3)]
                wfm = [sb(f"wfm{i}", [128, DC * 128], BF16) for i in range(2)]
                wtm = [sb(f"wtm{i}", [128, DC * 512], BF16) for i in range(2)]
                NE = 4
                e32 = [sb(f"e32{i}", [128, 512], F32) for i in range(NE)]
                ebf = [sb(f"ebf{i}", [128, 512], BF16) for i in range(NE)]
                f32b = [sb(f"f32b{i}", [128, 512], F32) for i in range(NE)]
                lf = [sb(f"lf{i}", [128, 512], F32) for i in range(NE)]
                om = [sb(f"om{i}", [128, 512], F32) for i in range(NE)]
                ltm = sb("ltm", [128, 2048], F32)
                oml_tm = sb("oml_tm", [128, 1024], F32)
                lb_tm = sb("lb_tm", [128, 1024], F32)
                k.dma("sp", ltm[:], lbl_tm, [], ["ltm"])
                k.tt("dve", oml_tm[:], ltm[:, 1024:2048], ltm[:, 0:1024], ALU.subtract, ["ltm"], ["omltm"])
                k.act(oml_tm[:], oml_tm[:], AF.Sigmoid, ["omltm"], ["omltm"])
                k.ts("dve", lb_tm[:], oml_tm[:], -1.0, 1.0, ALU.mult, ALU.add, ["omltm"], ["lbtm"])
                vmt = sb("vmt", [128, NB], F32)
                vmf = [sb(f"vmf{i}", [128, 512], F32) for i in range(2)]
                k.dma("sp", vmt[:], vm_tm_d, [], ["vmt"])
                qs = 1.0 / math.sqrt(128.0)
                ecnt = 0
                tmc = 0
                for ti, (t0, T) in enumerate(tilesA):
                    first = ti == 0
                    own = ti >= OWNT
                    for c0 in range(0, DC, 8):
                        k.dma("sp", xbf[:, c0:c0 + 8, :T], h1Tbf[c0:c0 + 8, :, t0:t0 + T].rearrange("c p t -> p c t"),
                              [f"h1_{ti}_{c}_1" for c in range(c0, c0 + 8)], ["xbf"])
                    jcnt = 0
                    for j in range(72):
                        if not own and (j // 8) != 4:
                            continue
                        s = jcnt % 2
                        jcnt += 1
                        fetch(wfm[s], f"wfm{s}", win_fm32[j], win_fmbf[j], f"winfm{j}", first, wst, DC * 128)
                        pp = ps[s]
                        for c in range(DC):
                            k.mm(pp[:, :T], wfm[s][:, c * 128:(c + 1) * 128], xbf[:, c, :T], c == 0, c == DC - 1,
                                 [f"wfm{s}", "xbf"], [f"ps{s}"], c == DC - 1)
                        e = ecnt % NE
                        ecnt += 1
                        grp, h = j // 8, j % 8
                        if grp == 0:
                            k.act(ebf[e][:, :T], pp[:, :T], AF.Silu, [f"ps{s}"], [f"ebf{e}"])
                            k.dma("pool", qT[h, :, t0:t0 + T], ebf[e][:, :T], [f"ebf{e}"], [f"qT{ti}"])
                        elif grp == 1:
                            k.act(e32[e][:, :T], pp[:, :T], AF.Sigmoid, [f"ps{s}"], [f"e32{e}"], scale=-1.0)
                            k.ts("dve", e32[e][:, :T], e32[e][:, :T], oml_fm[:, h:h + 1], None, ALU.mult, None,
                                 [f"e32{e}", "omlfm"], [f"e32{e}"])
                            if not own:
                                k.tt("dve", e32[e][:, :T], e32[e][:, :T], vmf[ti % 2][:, :T], ALU.mult,
                                     [f"e32{e}", f"vmf{ti % 2}"], [f"e32{e}"])
                            k.dma("pool", omfT[h, :, t0:t0 + T], e32[e][:, :T], [f"e32{e}"], [f"omfT{ti}"])
                        elif grp == 2:
                            k.act(ebf[e][:, :T], pp[:, :T], AF.Silu, [f"ps{s}"], [f"ebf{e}"])
                            k.dma("pool", ogT[h, :, t0:t0 + T], ebf[e][:, :T], [f"ebf{e}"], [f"ogT{ti}"])
                        elif grp == 3:
                            k.act(ebf[e][:, :T], pp[:, :T], AF.Copy, [f"ps{s}"], [f"ebf{e}"], scale=qs)
                            k.dma("pool", sqT[h, :, t0:t0 + T], ebf[e][:, :T], [f"ebf{e}"], [f"sqT{ti}"])
                        elif grp == 4:
                            k.act(ebf[e][:, :T], pp[:, :T], AF.Copy, [f"ps{s}"], [f"ebf{e}"])
                            k.dma("pool", skT[h, :, t0:t0 + T], ebf[e][:, :T], [f"ebf{e}"], [f"skT{ti}"])
                        else:
                            gc = j - 40
                            k.act(ebf[e][:, :T], pp[:, :T], AF.Sigmoid, [f"ps{s}", "bg"], [f"ebf{e}"],
                                  bias=bg_s[:, gc:gc + 1])
                            k.dma("pool", gT[gc, :, t0:t0 + T], ebf[e][:, :T], [f"ebf{e}"], [f"gT{ti}"])
                    for g in range(6):
                        s = g % 2
                        fetch(wtm[s], f"wtm{s}", win_tm32[g], win_tmbf[g], f"wintm{g}", first, wst, DC * 512)
                        for tb in range(T // 128):
                            pp = ps[2 + tmc % 4]
                            pk = f"ps{2 + tmc % 4}"
                            tmc += 1
                            for c in range(DC):
                                k.mm(pp[:, :], xbf[:, c, tb * 128:(tb + 1) * 128], wtm[s][:, c * 512:(c + 1) * 512],
                                     c == 0, c == DC - 1, [f"wtm{s}", "xbf"], [pk], c == DC - 1)
                            e = ecnt % NE
                            ecnt += 1
                            row0 = t0 + tb * 128
                            cs = slice((g % 2) * 512, (g % 2) * 512 + 512)
                            if g < 2:
                                k.act(f32b[e][:], pp[:, :], AF.Sigmoid, [pk], [f"f32b{e}"])
                                k.tt("dve", f32b[e][:], f32b[e][:], oml_tm[:, cs], ALU.mult, [f"f32b{e}", "omltm"], [f"f32b{e}"])
                                k.tt("dve", f32b[e][:], f32b[e][:], lb_tm[:, cs], ALU.add, [f"f32b{e}", "lbtm"], [f"f32b{e}"])
                                k.act(lf[e][:], f32b[e][:], AF.Ln, [f"f32b{e}"], [f"lf{e}"])
                                k.ts("dve", om[e][:], f32b[e][:], -1.0, 1.0, ALU.mult, ALU.add, [f"f32b{e}"], [f"om{e}"])
                                if not own:
                                    blk = row0 // 128
                                    k.ts("dve", lf[e][:], lf[e][:], vmt[:, blk:blk + 1], None, ALU.mult, None,
                                         [f"lf{e}", "vmt"], [f"lf{e}"])
                                    k.ts("dve", om[e][:], om[e][:], vmt[:, blk:blk + 1], None, ALU.mult, None,
                                         [f"om{e}", "vmt"], [f"om{e}"])
                                k.dma("pool", logf[row0:row0 + 128, cs], lf[e][:], [f"lf{e}"], [f"logf{ti}"])
                                k.dma("pool", omf[row0:row0 + 128, cs], om[e][:], [f"om{e}"], [f"omf{ti}"])
                            else:
                                k.act(ebf[e][:], pp[:, :], AF.Copy, [pk], [f"ebf{e}"])
                                if (not own) and g >= 4:
                                    blk = row0 // 128
                                    k.ts("dve", ebf[e][:], ebf[e][:], vmt[:, blk:blk + 1], None, ALU.mult, None,
                                         [f"ebf{e}", "vmt"], [f"ebf{e}"])
                                dst = vhg if g < 4 else svv
                                k.dma("pool", dst[row0:row0 + 128, cs], ebf[e][:], [f"ebf{e}"],
                                      [("vhg" if g < 4 else "svv") + str(ti)])
                S.barrier()
                S.emit()

        def phase_b2():
            S.barrier()
            with ExitStack() as st:
                def sb(n, shp, dt):
                    return st.enter_context(nc.sbuf_tensor(f"b2_{n}", shp, dt))
                NBUF = 2
                lf2 = [sb(f"lf2{i}", [64, 2, 512], F32) for i in range(NBUF)]
                om2 = [sb(f"om2{i}", [64, 2, 512], F32) for i in range(NBUF)]
                vv2 = [sb(f"vv2{i}", [64, 2, 512], BF16) for i in range(NBUF)]
                omT = [sb(f"omT{i}", [128, 4, 128], F32) for i in range(NBUF)]
                qTt = [sb(f"qTt{i}", [128, 4, 128], BF16) for i in range(NBUF)]
                ogt = [sb(f"ogt{i}", [128, 4, 128], BF16) for i in range(NBUF)]
                esfx = [sb(f"esfx{i}", [64, 2, 512], F32) for i in range(NBUF)]
                khat = [sb(f"khat{i}", [64, 2, 512], BF16) for i in range(NBUF)]
                Ep = [sb(f"Ep{i}", [128, 512], F32) for i in range(NBUF)]
                En = [sb(f"En{i}", [128, 512], F32) for i in range(NBUF)]
                QtT = [sb(f"QtT{i}", [128, 512], BF16) for i in range(NBUF)]
                KtT = [sb(f"KtT{i}", [128, 512], BF16) for i in range(NBUF)]
                AT = [sb(f"AT{i}", [64, 512], BF16) for i in range(NBUF)]
                Sst = sb("Sst", [128, NH * 128], F32)
                Sbf = sb("Sbf", [128, NH * 128], BF16)
                osq = [sb(f"osq{i}", [128, 512], BF16) for i in range(NBUF)]
                rs = [sb(f"rs{i}", [128, 512], F32) for i in range(NBUF)]
                o1 = [sb(f"o1{i}", [128, 512], F32) for i in range(NBUF)]
                o2 = [sb(f"o2{i}", [128, 4, 128], BF16) for i in range(NBUF)]
                onesbf_t = sb("onesbf", [128, 128], BF16)
                onesbf = onesbf_t[:]
                k.ms("dve", onesbf_t[:], 1.0, ["cbf"])
                k.ms("dve", Sst[:], 0.0, ["Sst0", "Sst1"])
                k.ms("dve", Sbf[:], 0.0, ["Sbf0", "Sbf1"])
                bg = []
                wstb = [sb(f"wstb{i}", [128, 2048], F32) for i in range(2)]
                cvb = [sb(f"cvb{i}", [128, FC * 128], BF16) for i in range(2)]

                def bg_slab(src32, scr, skey, n):
                    def f():
                        if skey in converted:
                            return
                        converted.add(skey)
                        z = bg_slab.ctr % 2
                        bg_slab.ctr += 1
                        for p0 in range(0, n, 2048):
                            m = min(2048, n - p0)
                            j = bg_slab.pc % 2
                            bg_slab.pc += 1
                            k.dma("sp", wstb[j][:, :m], src32[:, p0:p0 + m], [], [f"wstb{j}"])
                            k.cp("act", cvb[z][:, p0:p0 + m], wstb[j][:, :m], [f"wstb{j}"], [f"cvb{z}"])
                        k.dma("pool", scr, cvb[z][:, :n], [f"cvb{z}"], [skey])
                    return f
                bg_slab.ctr = 0
                bg_slab.pc = 0
                for c in range(DC):
                    bg.append(bg_slab(phg32[c], phgbf[c], f"phg{c}", 8 * 128))
                    bg.append(bg_slab(psb32[c], psbbf[c], f"psb{c}", 8 * 128))
                for c in range(DC):
                    bg.append(bg_slab(wout32[c], woutbf[c], f"wout{c}", DC * 128))
                W2 = Wi["f2"]
                for f_ in range(FC):
                    bg.append(bg_slab(W2["g32"][f_], W2["gbf"][f_], f"fcg{f_}", DC * 128))
                    bg.append(bg_slab(W2["u32"][f_], W2["ubf"][f_], f"fcu{f_}", DC * 128))
                for c in range(DC):
                    bg.append(bg_slab(W2["d32"][c], W2["dbf"][c], f"fcd{c}", FC * 128))
                bg_every = max(1, (2 * NB - 4) // (len(bg) + 1))
                bg_per = max(1, -(-len(bg) // max(1, 2 * NB - 4)))
                it = 0
                for g in range(NB):
                    bi = 0 if g == 0 else (g - 1) // 4 + 1
                    for hg in range(2):
                        for _ in range(bg_per):
                            if bg:
                                bg.pop(0)()
                        b = it % NBUF
                        it += 1
                        rows = slice(g * 128, (g + 1) * 128)
                        cols = slice(hg * 512, (hg + 1) * 512)
                        hs = slice(hg * 4, hg * 4 + 4)
                        k.dma("sp", lf2[b][:], logf[rows, cols].rearrange("(c s) k -> s c k", c=2), [f"logf{bi}"], [f"lf2{b}"])
                        k.dma("sp", om2[b][:], omf[rows, cols].rearrange("(c s) k -> s c k", c=2), [f"omf{bi}"], [f"om2{b}"])
                        k.dma("sp", vv2[b][:], vhg[rows, cols].rearrange("(c s) k -> s c k", c=2), [f"vhg{bi}"], [f"vv2{b}"])
                        own = g >= OWNB
                        if own:
                            k.dma("sp", omT[b][:], omfT[hs, :, rows].rearrange("h p t -> p h t"), [f"omfT{bi}"], [f"omT{b}"])
                            k.dma("sp", qTt[b][:], qT[hs, :, rows].rearrange("h p t -> p h t"), [f"qT{bi}"], [f"qTt{b}"])
                            k.dma("sp", ogt[b][:], ogT[hs, :, rows].rearrange("h p t -> p h t"), [f"ogT{bi}"], [f"ogt{b}"])
                        for c in range(2):
                            k.mm(ps[c][0:64, :], SU64, lf2[b][:, c, :], True, True, ["c32", f"lf2{b}"], [f"ps{c}"], True)
                        for h in range(4):
                            for c in range(2):
                                o0 = (h * 2 + c) * 64
                                k.mm(ps[2][:, o0:o0 + 64], lf2[b][:, c, h * 128:(h + 1) * 128], TRI64, True, True,
                                     ["c32", f"lf2{b}"], ["ps2"], h == 3 and c == 1)
                        for c in range(2):
                            k.act(esfx[b][:, c, :], ps[c][0:64, :], AF.Exp, [f"ps{c}"], [f"esfx{b}"])
                        k.tt("dve", khat[b][:], om2[b][:], esfx[b][:], ALU.mult, [f"om2{b}", f"esfx{b}"], [f"khat{b}"])
                        k.act(Ep[b][:], ps[2][:, :], AF.Exp, ["ps2"], [f"Ep{b}"])
                        if own:
                            k.act(En[b][:], ps[2][:, :], AF.Exp, ["ps2"], [f"En{b}"], scale=-1.0)
                            k.tt("dve", QtT[b][:], qTt[b][:].rearrange("p h t -> p (h t)"), Ep[b][:], ALU.mult,
                                 [f"qTt{b}", f"Ep{b}"], [f"QtT{b}"])
                            k.tt("dve", KtT[b][:], omT[b][:].rearrange("p h t -> p (h t)"), En[b][:], ALU.mult,
                                 [f"omT{b}", f"En{b}"], [f"KtT{b}"])
                            for h in range(4):
                                for c in range(2):
                                    o0 = (h * 2 + c) * 64
                                    k.mm(ps[3][0:64, o0:o0 + 64], KtT[b][:, o0:o0 + 64], QtT[b][:, o0:o0 + 64], True, True,
                                         [f"KtT{b}", f"QtT{b}"], ["ps3"], h == 3 and c == 1)
                            k.tt("dve", AT[b][:], ps[3][0:64, :], MASKLE, ALU.mult, ["ps3", "c32"], [f"AT{b}"])
                        sk = f"Sst{hg}"
                        sbk = f"Sbf{hg}"
                        for c in range(2):
                            for h in range(4 if own else 0):
                                o0 = (h * 2 + c) * 64
                                hh = hg * 4 + h
                                k.mm(ps[4][:, o0:o0 + 64], vv2[b][:, c, h * 128:(h + 1) * 128], AT[b][:, o0:o0 + 64],
                                     True, False, [f"vv2{b}", f"AT{b}"], ["ps4"], False)
                                k.mm(ps[4][:, o0:o0 + 64], Sbf[:, hh * 128:(hh + 1) * 128], QtT[b][:, o0:o0 + 64],
                                     False, True, [sbk, f"QtT{b}"], ["ps4"], h == 3)
                            for h in range(4):
                                k.mm(ps[5][:, h * 128:(h + 1) * 128], khat[b][:, c, h * 128:(h + 1) * 128],
                                     vv2[b][:, c, h * 128:(h + 1) * 128], True, True, [f"khat{b}", f"vv2{b}"], ["ps5"], h == 3)
                            for h in range(4):
                                hh = hg * 4 + h
                                col = (h * 2 + c) * 64 + 63
                                k.stt("dve", Sst[:, hh * 128:(hh + 1) * 128], Sst[:, hh * 128:(hh + 1) * 128],
                                      Ep[b][:, col:col + 1], ps[5][:, h * 128:(h + 1) * 128], ALU.mult, ALU.add,
                                      [sk, f"Ep{b}", "ps5"], [sk])
                            if g >= OWNB - 1:
                                k.cp("act", Sbf[:, hg * 512:(hg + 1) * 512], Sst[:, hg * 512:(hg + 1) * 512], [sk], [sbk])
                        if not own:
                            continue
                        k.act(osq[b][:], ps[4][:, :], AF.Square, ["ps4"], [f"osq{b}"])
                        k.mm(ps[6][:, :], onesbf, osq[b][:], True, True, ["cbf", f"osq{b}"], ["ps6"], True)
                        k.act(rs[b][:], ps[6][:, :], AF.Sqrt, ["ps6"], [f"rs{b}"], bias=RMS_EPS, scale=1.0 / 128.0)
                        k.recip(rs[b][:], rs[b][:], [f"rs{b}"], [f"rs{b}"])
                        k.tt("dve", o1[b][:], ps[4][:, :], rs[b][:], ALU.mult, ["ps4", f"rs{b}"], [f"o1{b}"])
                        for h in range(4):
                            hh = hg * 4 + h
                            k.stt("dve", o2[b][:, h, :], o1[b][:, h * 128:(h + 1) * 128], gn_s[:, hh:hh + 1], ogt[b][:, h, :],
                                  ALU.mult, ALU.mult, [f"o1{b}", "gn", f"ogt{b}"], [f"o2{b}"])
                        k.dma("pool", ohgT[hs, :, rows].rearrange("h p t -> p h t"), o2[b][:], [f"o2{b}"], [f"ohg{bi}"])
                while bg:
                    bg.pop(0)()
                S.barrier()
                S.emit()

        def phase_b3():
            S.barrier()
            with ExitStack() as st:
                def sb(n, shp, dt):
                    return st.enter_context(nc.sbuf_tensor(f"b3_{n}", shp, dt))
                KT = [sb(f"KT{i}", [128, L], BF16) for i in range(2)]
                VV = [sb(f"VV{i}", [128, NB, 128], BF16) for i in range(2)]
                QT = [sb(f"QT{i}", [128, 512], BF16) for i in range(3)]
                ee = [sb(f"ee{i}", [128, 512], F32) for i in range(2)]
                sp = [sb(f"sp{i}", [128, 512], BF16) for i in range(3)]
                accb = [sb(f"accb{i}", [128, 512], BF16) for i in range(2)]
                ww = [sb(f"ww{i}", [128, 512], BF16) for i in range(3)]
                oo = [sb(f"oo{i}", [128, 512], BF16) for i in range(2)]
                cBs = sb("cBs", [128, 4352], F32)
                cbf = sb("cbf", [128, 4352], BF16)
                nones_t = sb("nones", [128, 128], BF16)
                k.dma("sp", cBs[:], constsB, [], ["cBs"])
                k.cp("dve", cbf[:], cBs[:], ["cBs"], ["cbf"])
                k.ms("dve", nones_t[:], -1.0, ["cbf"])
                NUI = cbf[:, 0:128]
                IDENT = cbf[:, 128:256]
                NONES = nones_t[:]
                MJ = [cbf[:, 256 + 512 * j:768 + 512 * j] for j in range(4)]
                NEGM = [cbf[:, 2304 + 512 * j:2816 + 512 * j] for j in range(4)]
                allk = [f"skT{i}" for i in range(len(tilesA))]
                allv = [f"svv{i}" for i in range(len(tilesA))]
                def head_loads(h):
                    hb = h % 2
                    for b0 in range(0, NB, 8):
                        b1 = min(NB, b0 + 8)
                        k.dma("sp", VV[hb][:, b0:b1, :],
                              svv[b0 * 128:b1 * 128, h * 128:(h + 1) * 128].rearrange("(b s) d -> s b d", s=128),
                              allv, [f"VV{hb}"])
                    for c0 in range(0, L, 2048):
                        c1 = min(L, c0 + 2048)
                        k.dma("sp", KT[hb][:, c0:c1], skT[h, :, c0:c1], allk, [f"KT{hb}"])

                tl = []
                gcnt = 0
                for h in range(NH):
                    for qg in range(NTO, NT):
                        fb = 1 + 4 * qg
                        kbs = list(range(fb + 3, -1, -1))
                        gcnt += 1
                        for ki, kb in enumerate(kbs):
                            tl.append(dict(h=h, hb=h % 2, qg=qg, gp=gcnt % 2, g3=gcnt % 3, gi=gcnt - 1, ki=ki, kb=kb, n=len(kbs),
                                           dj=kb - fb, idx=len(tl)))

                groups = [(h_, qg_) for h_ in range(NH) for qg_ in range(NTO, NT)]

                def load_q(gi):
                    if gi >= len(groups):
                        return
                    h_, qg_ = groups[gi]
                    t0 = 128 + qg_ * 512
                    z = (gi + 1) % 3
                    k.dma("sp", QT[z][:, :], sqT[h_, :, t0:t0 + 512], [f"sqT{1 + qg_}"], [f"QT{z}"])

                def s1(t):
                    h, hb, qg, gp, ki, kb, idx = t["h"], t["hb"], t["qg"], t["gp"], t["ki"], t["kb"], t["idx"]
                    g3 = t["g3"]
                    if ki == 0:
                        if t["gi"] == 0:
                            load_q(0)
                        load_q(t["gi"] + 1)
                    b2, b3 = idx % 2, idx % 3
                    Kblk = KT[hb][:, kb * 128:(kb + 1) * 128]
                    k.mm(ps[b2][:, :], Kblk, QT[g3][:, :], True, True, [f"KT{hb}", f"QT{g3}"], [f"ps{b2}"], True)
                    k.act(ee[b2][:], ps[b2][:, :], AF.Exp, [f"ps{b2}"], [f"ee{b2}"])

                def s1b(t):
                    idx = t["idx"]
                    b2, b3 = idx % 2, idx % 3
                    k.act(sp[b3][:], ee[b2][:], AF.Ln, [f"ee{b2}"], [f"sp{b3}"], bias=1.0)
                    if t["dj"] >= 0:
                        k.tt("dve", sp[b3][:], sp[b3][:], MJ[t["dj"]], ALU.mult, [f"sp{b3}", "cbf"], [f"sp{b3}"])

                def s2(t):
                    h, hb, qg, gp, ki, kb, idx, n, dj = (t["h"], t["hb"], t["qg"], t["gp"], t["ki"], t["kb"], t["idx"],
                                                          t["n"], t["dj"])
                    b2, b3 = idx % 2, idx % 3
                    Kblk = KT[hb][:, kb * 128:(kb + 1) * 128]
                    cps, cpk = ps[4 + gp], f"ps{4 + gp}"
                    if ki < n - 1:
                        k.mm(cps[:, :], NONES, sp[b3][:], ki == 0, ki == n - 2, ["cbf", f"sp{b3}"], [cpk], True)
                        k.cp("dve", accb[b2][:], cps[:, :], [cpk], [f"accb{b2}"])
                    pl, plk = ps[2 + b2], f"ps{2 + b2}"
                    k.mm(pl[:, :], Kblk, QT[t["g3"]][:, :], True, False, [f"KT{hb}", f"QT{t['g3']}"], [plk], False)
                    if ki > 0:
                        k.mm(pl[:, :], IDENT, accb[(idx - 1) % 2][:], False, False, ["cbf", f"accb{(idx - 1) % 2}"], [plk], False)
                    if dj >= 0:
                        k.mm(pl[:, :], IDENT, NEGM[dj], False, False, ["cbf"], [plk], False)
                    k.mm(pl[:, :], NUI, sp[b3][:], False, True, ["cbf", f"sp{b3}"], [plk], True)
                    k.act(ww[b3][:], pl[:, :], AF.Exp, [plk], [f"ww{b3}"])

                def s3(t):
                    h, hb, qg, gp, ki, kb, idx, n = t["h"], t["hb"], t["qg"], t["gp"], t["ki"], t["kb"], t["idx"], t["n"]
                    b3 = idx % 3
                    po, pok = ps[6 + gp], f"ps{6 + gp}"
                    k.mm(po[:, :], VV[hb][:, kb, :], ww[b3][:], ki == 0, ki == n - 1, [f"VV{hb}", f"ww{b3}"], [pok], True)
                    if ki == n - 1:
                        t0 = 128 + qg * 512
                        k.act(oo[gp][:], po[:, :], AF.Copy, [pok], [f"oo{gp}"])
                        k.dma("pool", osbT[h, :, t0:t0 + 512], oo[gp][:], [f"oo{gp}"], [f"osb{1 + qg}"])

                ntl = len(tl)
                head_loads(0)
                if NH > 1:
                    head_loads(1)
                for i in range(ntl + 2):
                    if i < ntl:
                        s1(tl[i])
                        s1b(tl[i])
                    if 0 <= i - 1 < ntl:
                        s2(tl[i - 1])
                    if 0 <= i - 2 < ntl:
                        s3(tl[i - 2])
                        t = tl[i - 2]
                        if t["ki"] == 0 and t["qg"] == NTO and 1 <= t["h"] and t["h"] + 1 < NH:
                            head_loads(t["h"] + 1)
                S.barrier()
                S.emit()

        def phase_c1():
            S.barrier()
            with ExitStack() as st:
                def sb(n, shp, dt):
                    return st.enter_context(nc.sbuf_tensor(f"c1_{n}", shp, dt))
                ahg = sb("ahg", [128, 8, 512], BF16)
                asb = sb("asb", [128, 8, 512], BF16)
                gt = sb("gt", [128, 32, 512], BF16)
                yT = sb("yT", [128, DC, 512], BF16)
                wst = [sb(f"wst{i}", [128, 2048], F32) for i in range(3)]
                wp1 = [sb(f"wp1{i}", [128, 8 * 128], BF16) for i in range(2)]
                wp2 = [sb(f"wp2{i}", [128, 8 * 128], BF16) for i in range(2)]
                wo = [sb(f"wo{i}", [128, DC * 128], BF16) for i in range(2)]
                ta = [sb(f"ta{i}", [128, 512], F32) for i in range(2)]
                tb_ = [sb(f"tb{i}", [128, 512], F32) for i in range(2)]
                xs = [sb(f"xs{i}", [128, 512], F32) for i in range(2)]
                r = sb("r", [128, DC, 512], F32)
                sqb = [sb(f"sqb{i}", [128, 512], BF16) for i in range(2)]
                rhi = [sb(f"rhi{i}", [128, 512], BF16) for i in range(2)]
                rlo = [sb(f"rlo{i}", [128, 512], BF16) for i in range(2)]
                mean = sb("mean", [128, 512], F32)
                msq = sb("msq", [128, 512], F32)
                rstd = sb("rstd", [128, 512], F32)
                t1 = [sb(f"t1{i}", [128, 512], F32) for i in range(2)]
                yo = [sb(f"yo{i}", [128, 512], F32) for i in range(2)]
                yb = [sb(f"yb{i}", [128, 512], BF16) for i in range(2)]
                T = 512
                pending = []
                for ti, (o0, _) in enumerate(tilesC):
                    first = ti == 0
                    t0 = OWN0 + o0
                    ai = OWNT + ti
                    k.dma("sp", ahg[:], ohgT[:, :, t0:t0 + T].rearrange("h p t -> p h t"), [f"ohg{ai}"], ["ahg"])
                    k.dma("sp", asb[:], osbT[:, :, t0:t0 + T].rearrange("h p t -> p h t"), [f"osb{ai}"], ["asb"])
                    for c0 in range(0, 32, 8):
                        k.dma("sp", gt[:, c0:c0 + 8, :], gT[c0:c0 + 8, :, t0:t0 + T].rearrange("c p t -> p c t"), [f"gT{ai}"], ["gt"])
                    for c in range(DC):
                        s = c % 2
                        if pending:
                            pending.pop(0)()
                        fetch(wp1[s], f"wp1{s}", phg32[c], phgbf[c], f"phg{c}", first, wst, 8 * 128)
                        fetch(wp2[s], f"wp2{s}", psb32[c], psbbf[c], f"psb{c}", first, wst, 8 * 128)
                        for kc in range(8):
                            k.mm(ps[s][:, :], wp1[s][:, kc * 128:(kc + 1) * 128], ahg[:, kc, :], kc == 0, kc == 7,
                                 [f"wp1{s}", "ahg"], [f"ps{s}"], kc == 7)
                        for kc in range(8):
                            k.mm(ps[2 + s][:, :], wp2[s][:, kc * 128:(kc + 1) * 128], asb[:, kc, :], kc == 0, kc == 7,
                                 [f"wp2{s}", "asb"], [f"ps{2 + s}"], kc == 7)
                        k.tt("dve", ta[s][:], ps[s][:, :], gt[:, c, :], ALU.mult, [f"ps{s}", "gt"], [f"ta{s}"])
                        k.tt("dve", tb_[s][:], ps[2 + s][:, :], gt[:, 16 + c, :], ALU.mult, [f"ps{2 + s}", "gt"], [f"tb{s}"])
                        k.tt("dve", yT[:, c, :], ta[s][:], tb_[s][:], ALU.add, [f"ta{s}", f"tb{s}"], [f"yT{c}"])
                    while pending:
                        pending.pop(0)()

                    def stats(d):
                        z = d % 2
                        k.mm(ps[6][:, :], onesB, rhi[z][:], d == 0, False, [f"rhi{z}", "onesb"], ["ps6"], False)
                        k.mm(ps[6][:, :], onesB, rlo[z][:], False, d == DC - 1, [f"rlo{z}", "onesb"], ["ps6"], True)
                        k.mm(ps[7][:, :], onesB, sqb[z][:], d == 0, d == DC - 1, [f"sqb{z}", "onesb"], ["ps7"], True)

                    for c in range(DC):
                        s = c % 2
                        fetch(wo[s], f"wo{s}", wout32[c], woutbf[c], f"wout{c}", first, wst, DC * 128)
                        pm = ps[4 + s]
                        for kc in range(DC):
                            k.mm(pm[:, :], wo[s][:, kc * 128:(kc + 1) * 128], yT[:, kc, :], kc == 0, kc == DC - 1,
                                 [f"wo{s}", f"yT{kc}"], [f"ps{4 + s}"], kc == DC - 1)
                        if c > 0:
                            stats(c - 1)
                        k.dma("sp", xs[s][:], h1T[c, :, t0:t0 + T], [f"h1_{ai}_{c}_0"], [f"xs{s}"])
                        k.stt("dve", r[:, c, :], xs[s][:], ALPHA, pm[:, :], ALU.mult, ALU.add, [f"xs{s}", f"ps{4 + s}"], [f"r{c}"])
                        k.act(sqb[s][:], r[:, c, :], AF.Square, [f"r{c}"], [f"sqb{s}"])
                        k.cp("dve", rhi[s][:], r[:, c, :], [f"r{c}"], [f"rhi{s}"])
                        k.tt("dve", rlo[s][:], r[:, c, :], rhi[s][:], ALU.subtract, [f"r{c}", f"rhi{s}"], [f"rlo{s}"])
                    stats(DC - 1)
                    pending = ln_finish(r, T, LN_EPS, lnp_s[:, 2 * DC:3 * DC], lnp_s[:, 3 * DC:4 * DC], mean, msq, rstd, t1, yo, yb,
                                        lambda c, o0=o0: h2T[c, :, o0:o0 + T], lambda c, o0=o0: h2Tbf[c, :, o0:o0 + T],
                                        lambda c, b, ti=ti: f"h2_{ti}_{c}_{b}")
                while pending:
                    pending.pop(0)()
                S.barrier()
                S.emit()

        ffn_phase("fa", tilesA, xT, True, lambda ti, c, b: f"xin", xT, Wi["f1"], h1T, h1Tbf,
                  lambda ti, c, b: f"h1_{ti}_{c}_{b}", lnp_s[:, 0:DC], lnp_s[:, DC:2 * DC])
        phase_b1()
        phase_b2()
        phase_b3()
        phase_c1()
        ffn_phase("fc", tilesC, h2Tbf, False, lambda ti, c, b: f"h2_{ti}_{c}_{b}", h2T, Wi["f2"], outT, None,
                  lambda ti, c, b: f"out_{ti}_{c}", lnp_s[:, 4 * DC:5 * DC], lnp_s[:, 5 * DC:6 * DC])
        S.final_wait()
        S.emit()
    return nc


def _fm(W, kc, oc):
    return np.ascontiguousarray(W.reshape(kc, 128, oc, 128).transpose(2, 1, 0, 3)).reshape(oc, 128, kc * 128)


def _tm(W, kc, g):
    return np.ascontiguousarray(W.reshape(kc, 128, g, 512).transpose(2, 1, 0, 3)).reshape(g, 128, kc * 512)


def _consts():
    cA = np.zeros((128, 768), np.float32)
    cA[:, 0:128] = 1.0
    i = np.arange(64)
    cA[0:64, 128:192] = (i[:, None] > i[None, :])
    cA[0:64, 192:256] = (i[:, None] <= i[None, :])
    cA[0:64, 256:768] = np.tile((i[:, None] <= i[None, :]).astype(np.float32), (1, 8))
    cB = np.zeros((128, 4352), np.float32)
    p = np.arange(128)
    cB[:, 0:128] = -(p[:, None] >= p[None, :]).astype(np.float32)
    cB[:, 128:256] = np.eye(128, dtype=np.float32)
    t = np.arange(512)
    for j in range(4):
        m = ((j * 128 + p[:, None]) < t[None, :]).astype(np.float32)
        cB[:, 256 + 512 * j:768 + 512 * j] = m
        cB[:, 2304 + 512 * j:2816 + 512 * j] = -30000.0 * (1.0 - m)
    return cA, cB


def make_in_maps(inp, SEQ):
    f = lambda a: np.asarray(a, np.float32)
    x = f(inp["x"])
    B = x.shape[0]
    meta = f(inp["meta"])
    vec = lambda v: np.ascontiguousarray(f(v).reshape(-1, 128).T)
    lnp = np.concatenate([vec(inp[n][0]) for n in ("ln1_g", "ln1_b", "ln2_g", "ln2_b", "ln3_g", "ln3_b")], axis=1)
    w_in = f(inp["w_in"][0])
    fm_cols = np.concatenate([np.arange(0, 1024), np.arange(1024, 2048), np.arange(3072, 4096),
                              np.arange(4096, 5120), np.arange(5120, 6144), np.arange(7168, 11264)])
    tm_cols = np.concatenate([np.arange(1024, 2048), np.arange(2048, 3072), np.arange(6144, 7168)])
    lbl = f(inp["hg_lb_logits"])
    cA, cB = _consts()
    shared = {
        "constsA": cA, "constsB": cB,
        "lnp": np.ascontiguousarray(lnp),
        "bgate": vec(inp["b_gate"][0]),
        "gnorm": vec(inp["hg_norm_g"][0]),
        "lbl_fm": np.ascontiguousarray(np.concatenate([vec(lbl[0]), vec(lbl[1])], axis=1)),
        "lbl_tm": np.ascontiguousarray(np.broadcast_to(np.concatenate([lbl[0], lbl[1]])[None, :], (128, 2048))),
        "f1_g": _fm(f(inp["ffn1_w_gate"][0]), DC, FC), "f1_u": _fm(f(inp["ffn1_w_up"][0]), DC, FC),
        "f1_d": _fm(f(inp["ffn1_w_down"][0]), FC, DC),
        "f2_g": _fm(f(inp["ffn2_w_gate"][0]), DC, FC), "f2_u": _fm(f(inp["ffn2_w_up"][0]), DC, FC),
        "f2_d": _fm(f(inp["ffn2_w_down"][0]), FC, DC),
        "win_fm": _fm(np.ascontiguousarray(w_in[:, fm_cols]), DC, 72),
        "win_tm": _tm(np.ascontiguousarray(w_in[:, tm_cols]), DC, 6),
        "phg": _fm(f(inp["w_proj_hg"][0]), 8, DC), "psb": _fm(f(inp["w_proj_sb"][0]), 8, DC),
        "wout": _fm(f(inp["w_out"][0]), DC, DC),
    }
    maps = []
    L = SEQ + 128
    SO = SEQ // 2
    p = np.arange(128)
    for b in range(B):
        for j in range(2):
            xf = np.zeros((L, D), np.float32)
            if j == 1:
                xf[PADN:128] = meta
                xf[128:] = x[b]
                v0 = PADN
            else:
                xf[SO + PADN:SO + 128] = meta
                xf[SO + 128:] = x[b][:SO]
                v0 = SO + PADN
            valid = (np.arange(L) >= v0).astype(np.float32)
            m = dict(shared)
            m["xT"] = np.ascontiguousarray(xf.T).reshape(DC, 128, L)
            m["vm_tm"] = np.ascontiguousarray(valid.reshape(L // 128, 128).T)
            m["vm_fm"] = np.ascontiguousarray(np.broadcast_to(valid[None, :], (128, L)))
            maps.append(m)
    return maps


_CACHE = {}


def kernel(**inputs):
    x = np.asarray(inputs["x"])
    B, SEQ, _ = x.shape
    if SEQ not in _CACHE:
        _CACHE[SEQ] = build_program(SEQ)
    nc = _CACHE[SEQ]
    maps = make_in_maps(inputs, SEQ)
    res = run_bass_kernel_spmd(nc, maps, core_ids=list(range(2 * B)))
    out = np.empty((B, SEQ, D), np.float32)
    SO = SEQ // 2
    for b in range(B):
        for j in range(2):
            out[b, j * SO:(j + 1) * SO] = res.results[2 * b + j]["outT"].reshape(D, SO).T
    return out
```

```python
import math
from contextlib import ExitStack

import numpy as np
import concourse.bass as bass
import concourse.mybir as mybir
from concourse.bass_utils import run_bass_kernel_spmd

F32 = mybir.dt.float32
BF16 = mybir.dt.bfloat16
AF = mybir.ActivationFunctionType
ALU = mybir.AluOpType

D = 2048
DC = 16
FF = 5632
FC = 44
NH = 8
ALPHA = 2.0 ** 0.25
LN_EPS = 1e-5
RMS_EPS = 1e-6
NMETA = 16
PADN = 112
ENGS = ["pe", "act", "dve", "pool", "sp"]
NDS = 32
CH = 16000
NCHUNK = 8


class Sched:
    def __init__(self, nc, st):
        self.nc = nc
        self.ops = {e: [] for e in ENGS}
        self.cnt = {e: 0 for e in ENGS}
        self.res_w = {}
        self.res_r = {}
        self.waited = {e: {} for e in ENGS}
        self.ndma = 0
        self.ndma_q = [0, 0]
        self.dma_last = [0] * NDS
        self.esem = {e: [st.enter_context(nc.semaphore(f"s_{e}{i}")) for i in range(NCHUNK)] for e in ENGS[:4]}
        self.dsem = [st.enter_context(nc.semaphore(f"s_d{i}")) for i in range(NDS)]
        self.nops = 0

    def _need(self, eng, tok, waits):
        if tok[0] == "e":
            if tok[1] == eng and eng == "pe":
                return
            key = ("e", tok[1])
        else:
            key = ("d", tok[1])
        if self.waited[eng].get(key, 0) >= tok[2]:
            return
        self.waited[eng][key] = tok[2]
        waits.append(tok)

    def op(self, eng, fn, r=(), w=(), sig=True, dma=False):
        deps = []
        for k in r:
            t = self.res_w.get(k)
            if t is not None:
                deps.append(t)
        for k in w:
            t = self.res_w.get(k)
            if t is not None:
                deps.append(t)
            rr = self.res_r.get(k)
            if rr:
                for kk, v in rr.items():
                    if kk == "dma":
                        deps.extend(v)
                    else:
                        deps.append(v)
        waits = []
        for t in deps:
            self._need(eng, t, waits)
        if dma:
            half = NDS // 2
            qi = 0 if eng == "sp" else 1
            n = self.ndma_q[qi]
            self.ndma_q[qi] += 1
            k = qi * half + n % half
            if self.dma_last[k] > 0:
                self._need(eng, ("d", k, self.dma_last[k]), waits)
            v = self.dma_last[k] + 16
            self.ndma += 1
            self.dma_last[k] = v
            tok = ("d", k, v)
        else:
            if sig:
                self.cnt[eng] += 1
                tok = ("e", eng, self.cnt[eng])
            else:
                tok = ("e", eng, self.cnt[eng] + 1)
        for k in w:
            self.res_w[k] = tok
            self.res_r[k] = {}
        for k in r:
            d = self.res_r.setdefault(k, {})
            if dma:
                d.setdefault("dma", []).append(tok)
            else:
                d[eng] = tok
        self.ops[eng].append((waits, fn, tok if (sig or dma) else None))
        self.nops += 1

    def barrier(self):
        for e in ENGS:
            waits = []
            for e2 in ENGS[:4]:
                if e2 != e and self.cnt[e2] > 0:
                    self._need(e, ("e", e2, self.cnt[e2]), waits)
            for k in range(NDS):
                if self.dma_last[k] > 0:
                    self._need(e, ("d", k, self.dma_last[k]), waits)
            if waits:
                self.ops[e].append((waits, None, None))

    def _semval(self, tok):
        if tok[0] == "e":
            c = (tok[2] - 1) // CH
            assert c < NCHUNK, "semaphore chunks exhausted"
            return self.esem[tok[1]][c], (tok[2] - 1) % CH + 1
        return self.dsem[tok[1]], tok[2]

    def emit(self):
        nc = self.nc
        for e in ENGS[:4]:
            for (waits, fn, tok) in self.ops[e]:
                for t in waits:
                    if t[0] == "e":
                        assert t[2] <= self.cnt[t[1]], ("unsignaled dependency", e, t)
        with nc.Block() as block:
            def runner(name):
                def f(eng):
                    for (waits, fn, tok) in self.ops[name]:
                        for t in waits:
                            s, v = self._semval(t)
                            eng.wait_ge(s, v)
                        if fn is None:
                            continue
                        ins = fn(eng)
                        if tok is not None:
                            if tok[0] == "d":
                                ins.then_inc(self.dsem[tok[1]], 16)
                            else:
                                s, _ = self._semval(tok)
                                ins.then_inc(s, 1)
                return f
            block.sync(runner("sp"))
            block.tensor(runner("pe"))
            block.scalar(runner("act"))
            block.vector(runner("dve"))
            block.gpsimd(runner("pool"))
        for e in ENGS:
            self.ops[e] = []

    def final_wait(self):
        waits = []
        for k in range(NDS):
            if self.dma_last[k] > 0:
                self._need("sp", ("d", k, self.dma_last[k]), waits)
        for e2 in ENGS[:4]:
            if self.cnt[e2] > 0:
                self._need("sp", ("e", e2, self.cnt[e2]), waits)
        if waits:
            self.ops["sp"].append((waits, None, None))


class K:
    def __init__(self, S):
        self.S = S

    def mm(self, out, lhsT, rhs, start, stop, r, w, sig):
        self.S.op("pe", lambda e: e.matmul(out, lhsT, rhs, start=start, stop=stop), r, w, sig=sig)

    def act(self, out, in_, func, r, w, bias=None, scale=None):
        kw = {}
        if bias is not None:
            kw["bias"] = bias
        if scale is not None:
            kw["scale"] = scale
        self.S.op("act", lambda e: e.activation(out, in_, func, **kw), r, w)

    def tt(self, eng, out, in0, in1, op, r, w):
        self.S.op(eng, lambda e: e.tensor_tensor(out, in0, in1, op), r, w)

    def ts(self, eng, out, in0, s1, s2, op0, op1, r, w):
        if op1 is None:
            self.S.op(eng, lambda e: e.tensor_scalar(out, in0, s1, None, op0), r, w)
        else:
            self.S.op(eng, lambda e: e.tensor_scalar(out, in0, s1, s2, op0, op1), r, w)

    def stt(self, eng, out, in0, scalar, in1, op0, op1, r, w):
        self.S.op(eng, lambda e: e.scalar_tensor_tensor(out, in0, scalar, in1, op0, op1), r, w)

    def recip(self, out, in_, r, w):
        self.S.op("dve", lambda e: e.reciprocal(out, in_), r, w)

    def cp(self, eng, out, in_, r, w):
        if eng == "act":
            self.S.op(eng, lambda e: e.copy(out, in_), r, w)
        else:
            self.S.op(eng, lambda e: e.tensor_copy(out, in_), r, w)

    def ms(self, eng, ap, val, w):
        self.S.op(eng, lambda e: e.memset(ap, val), (), w)

    def dma(self, q, out, in_, r, w):
        self.S.op(q, lambda e: e.dma_start(out=out, in_=in_), r, w, dma=True)


def cdiv(a, b):
    return (a + b - 1) // b


def build_program(SEQ, debug=False):
    L = SEQ + 128
    NB = L // 128
    assert SEQ % 512 == 0
    NT = SEQ // 512
    assert NT % 2 == 0
    NTO = NT // 2
    OWN0 = 128 + 512 * NTO
    OWNB = OWN0 // 128
    OWNT = 1 + NTO
    SO = SEQ // 2
    tilesA = [(0, 128)] + [(128 + 512 * i, 512) for i in range(NT)]
    tilesC = [(512 * i, 512) for i in range(NTO)]

    nc = bass.Bass("TRN2", target_bir_lowering=False)

    def din(name, shape, dt=F32):
        return nc.dram_tensor(name, list(shape), dt, kind="ExternalInput").ap()

    def dscr(name, shape, dt):
        return nc.dram_tensor(name, list(shape), dt).ap()

    xT = din("xT", [DC, 128, L])
    constsA = din("constsA", [128, 768])
    constsB = din("constsB", [128, 4352])
    lnp = din("lnp", [128, 6 * DC])
    bgate = din("bgate", [128, 32])
    gnorm = din("gnorm", [128, NH])
    lbl_fm = din("lbl_fm", [128, 2 * NH])
    lbl_tm = din("lbl_tm", [128, 2 * 1024])
    vm_tm_d = din("vm_tm", [128, NB])
    vm_fm_d = din("vm_fm", [128, L])
    Wi = {}
    for nm in ("f1", "f2"):
        Wi[nm] = dict(g32=din(f"{nm}_g", [FC, 128, DC * 128]), u32=din(f"{nm}_u", [FC, 128, DC * 128]),
                      d32=din(f"{nm}_d", [DC, 128, FC * 128]),
                      gbf=dscr(f"{nm}_gbf", [FC, 128, DC * 128], BF16), ubf=dscr(f"{nm}_ubf", [FC, 128, DC * 128], BF16),
                      dbf=dscr(f"{nm}_dbf", [DC, 128, FC * 128], BF16))
    win_fm32 = din("win_fm", [72, 128, DC * 128])
    win_tm32 = din("win_tm", [6, 128, DC * 512])
    win_fmbf = dscr("win_fmbf", [72, 128, DC * 128], BF16)
    win_tmbf = dscr("win_tmbf", [6, 128, DC * 512], BF16)
    phg32 = din("phg", [DC, 128, 8 * 128])
    psb32 = din("psb", [DC, 128, 8 * 128])
    wout32 = din("wout", [DC, 128, DC * 128])
    phgbf = dscr("phgbf", [DC, 128, 8 * 128], BF16)
    psbbf = dscr("psbbf", [DC, 128, 8 * 128], BF16)
    woutbf = dscr("woutbf", [DC, 128, DC * 128], BF16)

    outT = nc.dram_tensor("outT", [DC, 128, SO], F32, kind="ExternalOutput").ap()

    kind_dbg = dict(kind="ExternalOutput") if debug else {}

    def dscr2(name, shape, dt):
        return nc.dram_tensor(name, list(shape), dt, **kind_dbg).ap()

    h1T = dscr2("h1T", [DC, 128, L], F32)
    h1Tbf = dscr("h1Tbf", [DC, 128, L], BF16)
    qT = dscr("qT", [NH, 128, L], BF16)
    ogT = dscr("ogT", [NH, 128, L], BF16)
    omfT = dscr("omfT", [NH, 128, L], F32)
    sqT = dscr("sqT", [NH, 128, L], BF16)
    skT = dscr("skT", [NH, 128, L], BF16)
    gT = dscr("gT", [32, 128, L], BF16)
    logf = dscr("logf", [L, 1024], F32)
    omf = dscr("omf", [L, 1024], F32)
    vhg = dscr("vhg", [L, 1024], BF16)
    svv = dscr("svv", [L, 1024], BF16)
    ohgT = dscr2("ohgT", [NH, 128, L], BF16)
    osbT = dscr2("osbT", [NH, 128, L], BF16)
    h2T = dscr2("h2T", [DC, 128, SO], F32)
    h2Tbf = dscr("h2Tbf", [DC, 128, SO], BF16)

    top = ExitStack()
    with top:
        top.enter_context(nc.allow_low_precision("bf16 matmul operands with fp32 accumulation"))
        top.enter_context(nc.allow_non_contiguous_dma("strided tile loads"))
        S = Sched(nc, top)
        k = K(S)
        ps = [top.enter_context(nc.psum_tensor(f"ps{i}", [128, 512], F32)) for i in range(8)]

        def psb_(n, shp, dt):
            return top.enter_context(nc.sbuf_tensor(n, shp, dt))

        c32 = psb_("c32", [128, 768], F32)
        lnp_s = psb_("lnp_s", [128, 6 * DC], F32)
        bg_s = psb_("bg_s", [128, 32], F32)
        gn_s = psb_("gn_s", [128, NH], F32)
        lfm = psb_("lfm", [128, 2 * NH], F32)
        oml_fm = psb_("oml_fm", [128, NH], F32)
        k.dma("sp", c32[:], constsA, [], ["c32"])
        k.dma("sp", lnp_s[:], lnp, [], ["lnp"])
        k.dma("sp", bg_s[:], bgate, [], ["bg"])
        k.dma("sp", gn_s[:], gnorm, [], ["gn"])
        k.dma("sp", lfm[:], lbl_fm, [], ["lfm"])
        ones32 = c32[:, 0:128]
        onesb_t = psb_("onesb", [128, 128], BF16)
        k.ms("dve", onesb_t[:], 1.0, ["onesb"])
        onesB = onesb_t[:]
        SU64 = c32[0:64, 128:192]
        TRI64 = c32[0:64, 192:256]
        MASKLE = c32[0:64, 256:768]
        k.tt("dve", oml_fm[:], lfm[:, NH:2 * NH], lfm[:, 0:NH], ALU.subtract, ["lfm"], ["omlfm"])
        k.act(oml_fm[:], oml_fm[:], AF.Sigmoid, ["omlfm"], ["omlfm"])

        converted = set()

        def fetch(dst, dkey, src32, scr, skey, first, wst, n):
            first = skey not in converted
            converted.add(skey)
            if first:
                ceng = fetch.rot[fetch.slab % len(fetch.rot)]
                fetch.slab += 1
                for pi, p0 in enumerate(range(0, n, 2048)):
                    m = min(2048, n - p0)
                    j = fetch.ctr % len(wst)
                    fetch.ctr += 1
                    k.dma("sp", wst[j][:, :m], src32[:, p0:p0 + m], [], [f"wst{j}"])
                    k.cp(ceng, dst[:, p0:p0 + m], wst[j][:, :m], [f"wst{j}"], [dkey])
                k.dma("pool", scr, dst[:, :n], [dkey], [skey])
            else:
                k.dma("sp", dst[:, :n], scr, [skey], [dkey])
        fetch.ctr = 0
        fetch.slab = 0
        fetch.rot = ["act", "pool", "act", "dve"]

        def ln_finish(r, T, eps_eff, g_ap, b_ap, mean, msq, rstd, t1, yo, yb, out32_fn, outbf_fn, okey_fn):
            k.act(mean[:, :T], ps[6][:, :T], AF.Copy, ["ps6"], ["mean"], scale=1.0 / D)
            k.tt("dve", msq[:, :T], mean[:, :T], mean[:, :T], ALU.mult, ["mean"], ["msq"])
            k.stt("dve", msq[:, :T], ps[7][:, :T], 1.0 / D, msq[:, :T], ALU.mult, ALU.subtract, ["ps7", "msq"], ["msq"])
            k.act(rstd[:, :T], msq[:, :T], AF.Sqrt, ["msq"], ["rstd"], bias=eps_eff)
            k.recip(rstd[:, :T], rstd[:, :T], ["rstd"], ["rstd"])
            def chunk(c):
                j = c % 2
                k.tt("dve", t1[j][:, :T], r[:, c, :T], mean[:, :T], ALU.subtract, [f"r{c}", "mean"], [f"t1{j}"])
                k.tt("dve", t1[j][:, :T], t1[j][:, :T], rstd[:, :T], ALU.mult, [f"t1{j}", "rstd"], [f"t1{j}"])
                k.ts("dve", yo[j][:, :T], t1[j][:, :T], g_ap[:, c:c + 1], b_ap[:, c:c + 1], ALU.mult, ALU.add,
                     [f"t1{j}", "lnp"], [f"yo{j}"])
                k.dma("pool", out32_fn(c), yo[j][:, :T], [f"yo{j}"], [okey_fn(c, 0)])
                if outbf_fn is not None:
                    k.cp("pool", yb[j][:, :T], yo[j][:, :T], [f"yo{j}"], [f"yb{j}"])
                    k.dma("pool", outbf_fn(c), yb[j][:, :T], [f"yb{j}"], [okey_fn(c, 1)])
            return [(lambda c=c: chunk(c)) for c in range(DC)]

        def ffn_phase(tag, tiles, x_src, x_is_f32, xkey, res_src, W, out32, outbf, okey, lg, lb_):
            S.barrier()
            with ExitStack() as st:
                def sb(n, shp, dt):
                    return st.enter_context(nc.sbuf_tensor(f"{tag}_{n}", shp, dt))
                xbfs = [sb(f"xbf{i}", [128, DC, 512], BF16) for i in range(2)]
                xp = [sb(f"xp{i}", [128, 512], F32) for i in range(2)]
                HT = sb("HT", [128, FC, 512], BF16)
                wst = [sb(f"wst{i}", [128, 2048], F32) for i in range(2)]
                wgb = [sb(f"wgb{i}", [128, DC * 128], BF16) for i in range(2)]
                wub = [sb(f"wub{i}", [128, DC * 128], BF16) for i in range(2)]
                wdb = [sb(f"wdb{i}", [128, FC * 128], BF16) for i in range(2)]
                xs = [sb(f"xs{i}", [128, 512], F32) for i in range(2)]
                sg = [sb(f"sg{i}", [128, 512], F32) for i in range(2)]
                r = sb("r", [128, DC, 512], F32)
                sqb = [sb(f"sqb{i}", [128, 512], BF16) for i in range(2)]
                rhi = [sb(f"rhi{i}", [128, 512], BF16) for i in range(2)]
                rlo = [sb(f"rlo{i}", [128, 512], BF16) for i in range(2)]
                mean = sb("mean", [128, 512], F32)
                msq = sb("msq", [128, 512], F32)
                rstd = sb("rstd", [128, 512], F32)
                t1 = [sb(f"t1{i}", [128, 512], F32) for i in range(2)]
                yo = [sb(f"yo{i}", [128, 512], F32) for i in range(2)]
                yb = [sb(f"yb{i}", [128, 512], BF16) for i in range(2)]
                def load_x(ti, oi):
                    t0, T = tiles[ti]
                    xb = xbfs[oi % 2]
                    p = oi % 2
                    if x_is_f32:
                        for c in range(DC):
                            j = c % 2
                            k.dma("sp", xp[j][:, :T], x_src[c, :, t0:t0 + T], [], [f"xp{j}"])
                            k.cp("dve", xb[:, c, :T], xp[j][:, :T], [f"xp{j}"], [f"xbf{p}_{c}"])
                    else:
                        for c0 in range(0, DC, 8):
                            k.dma("sp", xb[:, c0:c0 + 8, :T], x_src[c0:c0 + 8, :, t0:t0 + T].rearrange("c p t -> p c t"),
                                  [xkey(ti, c, 1) for c in range(c0, c0 + 8)], [f"xbf{p}_{c}" for c in range(c0, c0 + 8)])

                order = list(range(len(tiles)))
                if len(tiles) > 1 and tiles[0][1] < tiles[1][1]:
                    order[0], order[1] = 1, 0
                load_x(order[0], 0)
                pending = []
                for oi, ti in enumerate(order):
                    t0, T = tiles[ti]
                    first = oi == 0
                    xbf = xbfs[oi % 2]
                    xpar = oi % 2
                    for f in range(FC):
                        s = f % 2
                        if pending and f % 2 == 1:
                            pending.pop(0)()
                        fetch(wgb[s], f"wgb{s}", W["g32"][f], W["gbf"][f], f"{tag}g{f}", first, wst, DC * 128)
                        fetch(wub[s], f"wub{s}", W["u32"][f], W["ubf"][f], f"{tag}u{f}", first, wst, DC * 128)
                        pg, pu = ps[s], ps[2 + s]
                        for c in range(DC):
                            k.mm(pg[:, :T], wgb[s][:, c * 128:(c + 1) * 128], xbf[:, c, :T], c == 0, c == DC - 1,
                                 [f"wgb{s}", f"xbf{xpar}_{c}"], [f"ps{s}"], c == DC - 1)
                        for c in range(DC):
                            k.mm(pu[:, :T], wub[s][:, c * 128:(c + 1) * 128], xbf[:, c, :T], c == 0, c == DC - 1,
                                 [f"wub{s}", f"xbf{xpar}_{c}"], [f"ps{2 + s}"], c == DC - 1)
                        k.act(sg[s][:, :T], pg[:, :T], AF.Silu, [f"ps{s}"], [f"sg{s}"])
                        k.tt("dve", HT[:, f, :T], sg[s][:, :T], pu[:, :T], ALU.mult, [f"sg{s}", f"ps{2 + s}"], [f"HT{f}"])
                    while pending:
                        pending.pop(0)()
                    if oi + 1 < len(order):
                        load_x(order[oi + 1], oi + 1)

                    def stats(d, T=T):
                        z = d % 2
                        k.mm(ps[6][:, :T], onesB, rhi[z][:, :T], d == 0, False, [f"rhi{z}", "onesb"], ["ps6"], False)
                        k.mm(ps[6][:, :T], onesB, rlo[z][:, :T], False, d == DC - 1, [f"rlo{z}", "onesb"], ["ps6"], True)
                        k.mm(ps[7][:, :T], onesB, sqb[z][:, :T], d == 0, d == DC - 1, [f"sqb{z}", "onesb"], ["ps7"], True)

                    for dc in range(DC):
                        s = dc % 2
                        fetch(wdb[s], f"wdb{s}", W["d32"][dc], W["dbf"][dc], f"{tag}d{dc}", first, wst, FC * 128)
                        py = ps[4 + s]
                        for f in range(FC):
                            k.mm(py[:, :T], wdb[s][:, f * 128:(f + 1) * 128], HT[:, f, :T], f == 0, f == FC - 1,
                                 [f"wdb{s}", f"HT{f}"], [f"ps{4 + s}"], f == FC - 1)
                        if dc > 0:
                            stats(dc - 1)
                        k.dma("sp", xs[s][:, :T], res_src[dc, :, t0:t0 + T], [xkey(ti, dc, 0)], [f"xs{s}"])
                        k.stt("dve", r[:, dc, :T], xs[s][:, :T], 2.0 * ALPHA, py[:, :T], ALU.mult, ALU.add,
                              [f"xs{s}", f"ps{4 + s}"], [f"r{dc}"])
                        k.act(sqb[s][:, :T], r[:, dc, :T], AF.Square, [f"r{dc}"], [f"sqb{s}"])
                        k.cp("dve", rhi[s][:, :T], r[:, dc, :T], [f"r{dc}"], [f"rhi{s}"])
                        k.tt("dve", rlo[s][:, :T], r[:, dc, :T], rhi[s][:, :T], ALU.subtract, [f"r{dc}", f"rhi{s}"], [f"rlo{s}"])
                    stats(DC - 1)
                    pending = ln_finish(r, T, 4.0 * LN_EPS, lg, lb_, mean, msq, rstd, t1, yo, yb,
                                        lambda c, t0=t0, T=T: out32[c, :, t0:t0 + T],
                                        (lambda c, t0=t0, T=T: outbf[c, :, t0:t0 + T]) if outbf is not None else None,
                                        lambda c, b, ti=ti: okey(ti, c, b))
                while pending:
                    pending.pop(0)()
                S.barrier()
                S.emit()

        def phase_b1():
            S.barrier()
            with ExitStack() as st:
                def sb(n, shp, dt):
                    return st.enter_context(nc.sbuf_tensor(f"b1_{n}", shp, dt))
                xbf = sb("xbf", [128, DC, 512], BF16)
                wst = [sb(f"wst{i}", [128, 2048], F32) for i in range(3)]
                wfm = [sb(f"wfm{i}", [128, DC * 128], BF16) for i in range(2)]
                wtm = [sb(f"wtm{i}", [128, DC * 512], BF16) for i in range(2)]
                NE = 4
                e32 = [sb(f"e32{i}", [128, 512], F32) for i in range(NE)]
                ebf = [sb(f"ebf{i}", [128, 512], BF16) for i in range(NE)]
                f32b = [sb(f"f32b{i}", [128, 512], F32) for i in range(NE)]
                lf = [sb(f"lf{i}", [128, 512], F32) for i in range(NE)]
                om = [sb(f"om{i}", [128, 512], F32) for i in range(NE)]
                ltm = sb("ltm", [128, 2048], F32)
                oml_tm = sb("oml_tm", [128, 1024], F32)
                lb_tm = sb("lb_tm", [128, 1024], F32)
                k.dma("sp", ltm[:], lbl_tm, [], ["ltm"])
                k.tt("dve", oml_tm[:], ltm[:, 1024:2048], ltm[:, 0:1024], ALU.subtract, ["ltm"], ["omltm"])
                k.act(oml_tm[:], oml_tm[:], AF.Sigmoid, ["omltm"], ["omltm"])
                k.ts("dve", lb_tm[:], oml_tm[:], -1.0, 1.0, ALU.mult, ALU.add, ["omltm"], ["lbtm"])
                vmt = sb("vmt", [128, NB], F32)
                vmf = [sb(f"vmf{i}", [128, 512], F32) for i in range(2)]
                k.dma("sp", vmt[:], vm_tm_d, [], ["vmt"])
                qs = 1.0 / math.sqrt(128.0)
                ecnt = 0
                tmc = 0
                for ti, (t0, T) in enumerate(tilesA):
                    first = ti == 0
                    own = ti >= OWNT
                    for c0 in range(0, DC, 8):
                        k.dma("sp", xbf[:, c0:c0 + 8, :T], h1Tbf[c0:c0 + 8, :, t0:t0 + T].rearrange("c p t -> p c t"),
                              [f"h1_{ti}_{c}_1" for c in range(c0, c0 + 8)], ["xbf"])
                    jcnt = 0
                    for j in range(72):
                        if not own and (j // 8) != 4:
                            continue
                        s = jcnt % 2
                        jcnt += 1
                        fetch(wfm[s], f"wfm{s}", win_fm32[j], win_fmbf[j], f"winfm{j}", first, wst, DC * 128)
                        pp = ps[s]
                        for c in range(DC):
                            k.mm(pp[:, :T], wfm[s][:, c * 128:(c + 1) * 128], xbf[:, c, :T], c == 0, c == DC - 1,
                                 [f"wfm{s}", "xbf"], [f"ps{s}"], c == DC - 1)
                        e = ecnt % NE
                        ecnt += 1
                        grp, h = j // 8, j % 8
                        if grp == 0:
                            k.act(ebf[e][:, :T], pp[:, :T], AF.Silu, [f"ps{s}"], [f"ebf{e}"])
                            k.dma("pool", qT[h, :, t0:t0 + T], ebf[e][:, :T], [f"ebf{e}"], [f"qT{ti}"])
                        elif grp == 1:
                            k.act(e32[e][:, :T], pp[:, :T], AF.Sigmoid, [f"ps{s}"], [f"e32{e}"], scale=-1.0)
                            k.ts("dve", e32[e][:, :T], e32[e][:, :T], oml_fm[:, h:h + 1], None, ALU.mult, None,
                                 [f"e32{e}", "omlfm"], [f"e32{e}"])
                            if not own:
                                k.tt("dve", e32[e][:, :T], e32[e][:, :T], vmf[ti % 2][:, :T], ALU.mult,
                                     [f"e32{e}", f"vmf{ti % 2}"], [f"e32{e}"])
                            k.dma("pool", omfT[h, :, t0:t0 + T], e32[e][:, :T], [f"e32{e}"], [f"omfT{ti}"])
                        elif grp == 2:
                            k.act(ebf[e][:, :T], pp[:, :T], AF.Silu, [f"ps{s}"], [f"ebf{e}"])
                            k.dma("pool", ogT[h, :, t0:t0 + T], ebf[e][:, :T], [f"ebf{e}"], [f"ogT{ti}"])
                        elif grp == 3:
                            k.act(ebf[e][:, :T], pp[:, :T], AF.Copy, [f"ps{s}"], [f"ebf{e}"], scale=qs)
                            k.dma("pool", sqT[h, :, t0:t0 + T], ebf[e][:, :T], [f"ebf{e}"], [f"sqT{ti}"])
                        elif grp == 4:
                            k.act(ebf[e][:, :T], pp[:, :T], AF.Copy, [f"ps{s}"], [f"ebf{e}"])
                            k.dma("pool", skT[h, :, t0:t0 + T], ebf[e][:, :T], [f"ebf{e}"], [f"skT{ti}"])
                        else:
                            gc = j - 40
                            k.act(ebf[e][:, :T], pp[:, :T], AF.Sigmoid, [f"ps{s}", "bg"], [f"ebf{e}"],
                                  bias=bg_s[:, gc:gc + 1])
                            k.dma("pool", gT[gc, :, t0:t0 + T], ebf[e][:, :T], [f"ebf{e}"], [f"gT{ti}"])
                    for g in range(6):
                        s = g % 2
                        fetch(wtm[s], f"wtm{s}", win_tm32[g], win_tmbf[g], f"wintm{g}", first, wst, DC * 512)
                        for tb in range(T // 128):
                            pp = ps[2 + tmc % 4]
                            pk = f"ps{2 + tmc % 4}"
                            tmc += 1
                            for c in range(DC):
                                k.mm(pp[:, :], xbf[:, c, tb * 128:(tb + 1) * 128], wtm[s][:, c * 512:(c + 1) * 512],
                                     c == 0, c == DC - 1, [f"wtm{s}", "xbf"], [pk], c == DC - 1)
                            e = ecnt % NE
                            ecnt += 1
                            row0 = t0 + tb * 128
                            cs = slice((g % 2) * 512, (g % 2) * 512 + 512)
                            if g < 2:
                                k.act(f32b[e][:], pp[:, :], AF.Sigmoid, [pk], [f"f32b{e}"])
                                k.tt("dve", f32b[e][:], f32b[e][:], oml_tm[:, cs], ALU.mult, [f"f32b{e}", "omltm"], [f"f32b{e}"])
                                k.tt("dve", f32b[e][:], f32b[e][:], lb_tm[:, cs], ALU.add, [f"f32b{e}", "lbtm"], [f"f32b{e}"])
                                k.act(lf[e][:], f32b[e][:], AF.Ln, [f"f32b{e}"], [f"lf{e}"])
                                k.ts("dve", om[e][:], f32b[e][:], -1.0, 1.0, ALU.mult, ALU.add, [f"f32b{e}"], [f"om{e}"])
                                if not own:
                                    blk = row0 // 128
                                    k.ts("dve", lf[e][:], lf[e][:], vmt[:, blk:blk + 1], None, ALU.mult, None,
                                         [f"lf{e}", "vmt"], [f"lf{e}"])
                                    k.ts("dve", om[e][:], om[e][:], vmt[:, blk:blk + 1], None, ALU.mult, None,
                                         [f"om{e}", "vmt"], [f"om{e}"])
                                k.dma("pool", logf[row0:row0 + 128, cs], lf[e][:], [f"lf{e}"], [f"logf{ti}"])
                                k.dma("pool", omf[row0:row0 + 128, cs], om[e][:], [f"om{e}"], [f"omf{ti}"])
                            else:
                                k.act(ebf[e][:], pp[:, :], AF.Copy, [pk], [f"ebf{e}"])
                                if (not own) and g >= 4:
                                    blk = row0 // 128
                                    k.ts("dve", ebf[e][:], ebf[e][:], vmt[:, blk:blk + 1], None, ALU.mult, None,
                                         [f"ebf{e}", "vmt"], [f"ebf{e}"])
                                dst = vhg if g < 4 else svv
                                k.dma("pool", dst[row0:row0 + 128, cs], ebf[e][:], [f"ebf{e}"],
                                      [("vhg" if g < 4 else "svv") + str(ti)])
                S.barrier()
                S.emit()

        def phase_b2():
            S.barrier()
            with ExitStack() as st:
                def sb(n, shp, dt):
                    return st.enter_context(nc.sbuf_tensor(f"b2_{n}", shp, dt))
                NBUF = 2
                lf2 = [sb(f"lf2{i}", [64, 2, 512], F32) for i in range(NBUF)]
                om2 = [sb(f"om2{i}", [64, 2, 512], F32) for i in range(NBUF)]
                vv2 = [sb(f"vv2{i}", [64, 2, 512], BF16) for i in range(NBUF)]
                omT = [sb(f"omT{i}", [128, 4, 128], F32) for i in range(NBUF)]
                qTt = [sb(f"qTt{i}", [128, 4, 128], BF16) for i in range(NBUF)]
                ogt = [sb(f"ogt{i}", [128, 4, 128], BF16) for i in range(NBUF)]
                esfx = [sb(f"esfx{i}", [64, 2, 512], F32) for i in range(NBUF)]
                khat = [sb(f"khat{i}", [64, 2, 512], BF16) for i in range(NBUF)]
                Ep = [sb(f"Ep{i}", [128, 512], F32) for i in range(NBUF)]
                En = [sb(f"En{i}", [128, 512], F32) for i in range(NBUF)]
                QtT = [sb(f"QtT{i}", [128, 512], BF16) for i in range(NBUF)]
                KtT = [sb(f"KtT{i}", [128, 512], BF16) for i in range(NBUF)]
                AT = [sb(f"AT{i}", [64, 512], BF16) for i in range(NBUF)]
                Sst = sb("Sst", [128, NH * 128], F32)
                Sbf = sb("Sbf", [128, NH * 128], BF16)
                osq = [sb(f"osq{i}", [128, 512], BF16) for i in range(NBUF)]
                rs = [sb(f"rs{i}", [128, 512], F32) for i in range(NBUF)]
                o1 = [sb(f"o1{i}", [128, 512], F32) for i in range(NBUF)]
                o2 = [sb(f"o2{i}", [128, 4, 128], BF16) for i in range(NBUF)]
                onesbf_t = sb("onesbf", [128, 128], BF16)
                onesbf = onesbf_t[:]
                k.ms("dve", onesbf_t[:], 1.0, ["cbf"])
                k.ms("dve", Sst[:], 0.0, ["Sst0", "Sst1"])
                k.ms("dve", Sbf[:], 0.0, ["Sbf0", "Sbf1"])
                bg = []
                wstb = [sb(f"wstb{i}", [128, 2048], F32) for i in range(2)]
                cvb = [sb(f"cvb{i}", [128, FC * 128], BF16) for i in range(2)]

                def bg_slab(src32, scr, skey, n):
                    def f():
                        if skey in converted:
                            return
                        converted.add(skey)
                        z = bg_slab.ctr % 2
                        bg_slab.ctr += 1
                        for p0 in range(0, n, 2048):
                            m = min(2048, n - p0)
                            j = bg_slab.pc % 2
                            bg_slab.pc += 1
                            k.dma("sp", wstb[j][:, :m], src32[:, p0:p0 + m], [], [f"wstb{j}"])
                            k.cp("act", cvb[z][:, p0:p0 + m], wstb[j][:, :m], [f"wstb{j}"], [f"cvb{z}"])
                        k.dma("pool", scr, cvb[z][:, :n], [f"cvb{z}"], [skey])
                    return f
                bg_slab.ctr = 0
                bg_slab.pc = 0
                for c in range(DC):
                    bg.append(bg_slab(phg32[c], phgbf[c], f"phg{c}", 8 * 128))
                    bg.append(bg_slab(psb32[c], psbbf[c], f"psb{c}", 8 * 128))
                for c in range(DC):
                    bg.append(bg_slab(wout32[c], woutbf[c], f"wout{c}", DC * 128))
                W2 = Wi["f2"]
                for f_ in range(FC):
                    bg.append(bg_slab(W2["g32"][f_], W2["gbf"][f_], f"fcg{f_}", DC * 128))
                    bg.append(bg_slab(W2["u32"][f_], W2["ubf"][f_], f"fcu{f_}", DC * 128))
                for c in range(DC):
                    bg.append(bg_slab(W2["d32"][c], W2["dbf"][c], f"fcd{c}", FC * 128))
                bg_every = max(1, (2 * NB - 4) // (len(bg) + 1))
                bg_per = max(1, -(-len(bg) // max(1, 2 * NB - 4)))
                it = 0
                for g in range(NB):
                    bi = 0 if g == 0 else (g - 1) // 4 + 1
                    for hg in range(2):
                        for _ in range(bg_per):
                            if bg:
                                bg.pop(0)()
                        b = it % NBUF
                        it += 1
                        rows = slice(g * 128, (g + 1) * 128)
                        cols = slice(hg * 512, (hg + 1) * 512)
                        hs = slice(hg * 4, hg * 4 + 4)
                        k.dma("sp", lf2[b][:], logf[rows, cols].rearrange("(c s) k -> s c k", c=2), [f"logf{bi}"], [f"lf2{b}"])
                        k.dma("sp", om2[b][:], omf[rows, cols].rearrange("(c s) k -> s c k", c=2), [f"omf{bi}"], [f"om2{b}"])
                        k.dma("sp", vv2[b][:], vhg[rows, cols].rearrange("(c s) k -> s c k", c=2), [f"vhg{bi}"], [f"vv2{b}"])
                        own = g >= OWNB
                        if own:
                            k.dma("sp", omT[b][:], omfT[hs, :, rows].rearrange("h p t -> p h t"), [f"omfT{bi}"], [f"omT{b}"])
                            k.dma("sp", qTt[b][:], qT[hs, :, rows].rearrange("h p t -> p h t"), [f"qT{bi}"], [f"qTt{b}"])
                            k.dma("sp", ogt[b][:], ogT[hs, :, rows].rearrange("h p t -> p h t"), [f"ogT{bi}"], [f"ogt{b}"])
                        for c in range(2):
                            k.mm(ps[c][0:64, :], SU64, lf2[b][:, c, :], True, True, ["c32", f"lf2{b}"], [f"ps{c}"], True)
                        for h in range(4):
                            for c in range(2):
                                o0 = (h * 2 + c) * 64
                                k.mm(ps[2][:, o0:o0 + 64], lf2[b][:, c, h * 128:(h + 1) * 128], TRI64, True, True,
                                     ["c32", f"lf2{b}"], ["ps2"], h == 3 and c == 1)
                        for c in range(2):
                            k.act(esfx[b][:, c, :], ps[c][0:64, :], AF.Exp, [f"ps{c}"], [f"esfx{b}"])
                        k.tt("dve", khat[b][:], om2[b][:], esfx[b][:], ALU.mult, [f"om2{b}", f"esfx{b}"], [f"khat{b}"])
                        k.act(Ep[b][:], ps[2][:, :], AF.Exp, ["ps2"], [f"Ep{b}"])
                        if own:
                            k.act(En[b][:], ps[2][:, :], AF.Exp, ["ps2"], [f"En{b}"], scale=-1.0)
                            k.tt("dve", QtT[b][:], qTt[b][:].rearrange("p h t -> p (h t)"), Ep[b][:], ALU.mult,
                                 [f"qTt{b}", f"Ep{b}"], [f"QtT{b}"])
                            k.tt("dve", KtT[b][:], omT[b][:].rearrange("p h t -> p (h t)"), En[b][:], ALU.mult,
                                 [f"omT{b}", f"En{b}"], [f"KtT{b}"])
                            for h in range(4):
                                for c in range(2):
                                    o0 = (h * 2 + c) * 64
                                    k.mm(ps[3][0:64, o0:o0 + 64], KtT[b][:, o0:o0 + 64], QtT[b][:, o0:o0 + 64], True, True,
                                         [f"KtT{b}", f"QtT{b}"], ["ps3"], h == 3 and c == 1)
                            k.tt("dve", AT[b][:], ps[3][0:64, :], MASKLE, ALU.mult, ["ps3", "c32"], [f"AT{b}"])
                        sk = f"Sst{hg}"
                        sbk = f"Sbf{hg}"
                        for c in range(2):
                            for h in range(4 if own else 0):
                                o0 = (h * 2 + c) * 64
                                hh = hg * 4 + h
                                k.mm(ps[4][:, o0:o0 + 64], vv2[b][:, c, h * 128:(h + 1) * 128], AT[b][:, o0:o0 + 64],
                                     True, False, [f"vv2{b}", f"AT{b}"], ["ps4"], False)
                                k.mm(ps[4][:, o0:o0 + 64], Sbf[:, hh * 128:(hh + 1) * 128], QtT[b][:, o0:o0 + 64],
                                     False, True, [sbk, f"QtT{b}"], ["ps4"], h == 3)
                            for h in range(4):
                                k.mm(ps[5][:, h * 128:(h + 1) * 128], khat[b][:, c, h * 128:(h + 1) * 128],
                                     vv2[b][:, c, h * 128:(h + 1) * 128], True, True, [f"khat{b}", f"vv2{b}"], ["ps5"], h == 3)
                            for h in range(4):
                                hh = hg * 4 + h
                                col = (h * 2 + c) * 64 + 63
                                k.stt("dve", Sst[:, hh * 128:(hh + 1) * 128], Sst[:, hh * 128:(hh + 1) * 128],
                                      Ep[b][:, col:col + 1], ps[5][:, h * 128:(h + 1) * 128], ALU.mult, ALU.add,
                                      [sk, f"Ep{b}", "ps5"], [sk])
                            if g >= OWNB - 1:
                                k.cp("act", Sbf[:, hg * 512:(hg + 1) * 512], Sst[:, hg * 512:(hg + 1) * 512], [sk], [sbk])
                        if not own:
                            continue
                        k.act(osq[b][:], ps[4][:, :], AF.Square, ["ps4"], [f"osq{b}"])
                        k.mm(ps[6][:, :], onesbf, osq[b][:], True, True, ["cbf", f"osq{b}"], ["ps6"], True)
                        k.act(rs[b][:], ps[6][:, :], AF.Sqrt, ["ps6"], [f"rs{b}"], bias=RMS_EPS, scale=1.0 / 128.0)
                        k.recip(rs[b][:], rs[b][:], [f"rs{b}"], [f"rs{b}"])
                        k.tt("dve", o1[b][:], ps[4][:, :], rs[b][:], ALU.mult, ["ps4", f"rs{b}"], [f"o1{b}"])
                        for h in range(4):
                            hh = hg * 4 + h
                            k.stt("dve", o2[b][:, h, :], o1[b][:, h * 128:(h + 1) * 128], gn_s[:, hh:hh + 1], ogt[b][:, h, :],
                                  ALU.mult, ALU.mult, [f"o1{b}", "gn", f"ogt{b}"], [f"o2{b}"])
                        k.dma("pool", ohgT[hs, :, rows].rearrange("h p t -> p h t"), o2[b][:], [f"o2{b}"], [f"ohg{bi}"])
                while bg:
                    bg.pop(0)()
                S.barrier()
                S.emit()

        def phase_b3():
            S.barrier()
            with ExitStack() as st:
                def sb(n, shp, dt):
                    return st.enter_context(nc.sbuf_tensor(f"b3_{n}", shp, dt))
                KT = [sb(f"KT{i}", [128, L], BF16) for i in range(2)]
                VV = [sb(f"VV{i}", [128, NB, 128], BF16) for i in range(2)]
                QT = [sb(f"QT{i}", [128, 512], BF16) for i in range(3)]
                ee = [sb(f"ee{i}", [128, 512], F32) for i in range(2)]
                sp = [sb(f"sp{i}", [128, 512], BF16) for i in range(3)]
                accb = [sb(f"accb{i}", [128, 512], BF16) for i in range(2)]
                ww = [sb(f"ww{i}", [128, 512], BF16) for i in range(3)]
                oo = [sb(f"oo{i}", [128, 512], BF16) for i in range(2)]
                cBs = sb("cBs", [128, 4352], F32)
                cbf = sb("cbf", [128, 4352], BF16)
                nones_t = sb("nones", [128, 128], BF16)
                k.dma("sp", cBs[:], constsB, [], ["cBs"])
                k.cp("dve", cbf[:], cBs[:], ["cBs"], ["cbf"])
                k.ms("dve", nones_t[:], -1.0, ["cbf"])
                NUI = cbf[:, 0:128]
                IDENT = cbf[:, 128:256]
                NONES = nones_t[:]
                MJ = [cbf[:, 256 + 512 * j:768 + 512 * j] for j in range(4)]
                NEGM = [cbf[:, 2304 + 512 * j:2816 + 512 * j] for j in range(4)]
                allk = [f"skT{i}" for i in range(len(tilesA))]
                allv = [f"svv{i}" for i in range(len(tilesA))]
                def head_loads(h):
                    hb = h % 2
                    for b0 in range(0, NB, 8):
                        b1 = min(NB, b0 + 8)
                        k.dma("sp", VV[hb][:, b0:b1, :],
                              svv[b0 * 128:b1 * 128, h * 128:(h + 1) * 128].rearrange("(b s) d -> s b d", s=128),
                              allv, [f"VV{hb}"])
                    for c0 in range(0, L, 2048):
                        c1 = min(L, c0 + 2048)
                        k.dma("sp", KT[hb][:, c0:c1], skT[h, :, c0:c1], allk, [f"KT{hb}"])

                tl = []
                gcnt = 0
                for h in range(NH):
                    for qg in range(NTO, NT):
                        fb = 1 + 4 * qg
                        kbs = list(range(fb + 3, -1, -1))
                        gcnt += 1
                        for ki, kb in enumerate(kbs):
                            tl.append(dict(h=h, hb=h % 2, qg=qg, gp=gcnt % 2, g3=gcnt % 3, gi=gcnt - 1, ki=ki, kb=kb, n=len(kbs),
                                           dj=kb - fb, idx=len(tl)))

                groups = [(h_, qg_) for h_ in range(NH) for qg_ in range(NTO, NT)]

                def load_q(gi):
                    if gi >= len(groups):
                        return
                    h_, qg_ = groups[gi]
                    t0 = 128 + qg_ * 512
                    z = (gi + 1) % 3
                    k.dma("sp", QT[z][:, :], sqT[h_, :, t0:t0 + 512], [f"sqT{1 + qg_}"], [f"QT{z}"])

                def s1(t):
                    h, hb, qg, gp, ki, kb, idx = t["h"], t["hb"], t["qg"], t["gp"], t["ki"], t["kb"], t["idx"]
                    g3 = t["g3"]
                    if ki == 0:
                        if t["gi"] == 0:
                            load_q(0)
                        load_q(t["gi"] + 1)
                    b2, b3 = idx % 2, idx % 3
                    Kblk = KT[hb][:, kb * 128:(kb + 1) * 128]
                    k.mm(ps[b2][:, :], Kblk, QT[g3][:, :], True, True, [f"KT{hb}", f"QT{g3}"], [f"ps{b2}"], True)
                    k.act(ee[b2][:], ps[b2][:, :], AF.Exp, [f"ps{b2}"], [f"ee{b2}"])

                def s1b(t):
                    idx = t["idx"]
                    b2, b3 = idx % 2, idx % 3
                    k.act(sp[b3][:], ee[b2][:], AF.Ln, [f"ee{b2}"], [f"sp{b3}"], bias=1.0)
                    if t["dj"] >= 0:
                        k.tt("dve", sp[b3][:], sp[b3][:], MJ[t["dj"]], ALU.mult, [f"sp{b3}", "cbf"], [f"sp{b3}"])

                def s2(t):
                    h, hb, qg, gp, ki, kb, idx, n, dj = (t["h"], t["hb"], t["qg"], t["gp"], t["ki"], t["kb"], t["idx"],
                                                          t["n"], t["dj"])
                    b2, b3 = idx % 2, idx % 3
                    Kblk = KT[hb][:, kb * 128:(kb + 1) * 128]
                    cps, cpk = ps[4 + gp], f"ps{4 + gp}"
                    if ki < n - 1:
                        k.mm(cps[:, :], NONES, sp[b3][:], ki == 0, ki == n - 2, ["cbf", f"sp{b3}"], [cpk], True)
                        k.cp("dve", accb[b2][:], cps[:, :], [cpk], [f"accb{b2}"])
                    pl, plk = ps[2 + b2], f"ps{2 + b2}"
                    k.mm(pl[:, :], Kblk, QT[t["g3"]][:, :], True, False, [f"KT{hb}", f"QT{t['g3']}"], [plk], False)
                    if ki > 0:
                        k.mm(pl[:, :], IDENT, accb[(idx - 1) % 2][:], False, False, ["cbf", f"accb{(idx - 1) % 2}"], [plk], False)
                    if dj >= 0:
                        k.mm(pl[:, :], IDENT, NEGM[dj], False, False, ["cbf"], [plk], False)
                    k.mm(pl[:, :], NUI, sp[b3][:], False, True, ["cbf", f"sp{b3}"], [plk], True)
                    k.act(ww[b3][:], pl[:, :], AF.Exp, [plk], [f"ww{b3}"])

                def s3(t):
                    h, hb, qg, gp, ki, kb, idx, n = t["h"], t["hb"], t["qg"], t["gp"], t["ki"], t["kb"], t["idx"], t["n"]
                    b3 = idx % 3
                    po, pok = ps[6 + gp], f"ps{6 + gp}"
                    k.mm(po[:, :], VV[hb][:, kb, :], ww[b3][:], ki == 0, ki == n - 1, [f"VV{hb}", f"ww{b3}"], [pok], True)
                    if ki == n - 1:
                        t0 = 128 + qg * 512
                        k.act(oo[gp][:], po[:, :], AF.Copy, [pok], [f"oo{gp}"])
                        k.dma("pool", osbT[h, :, t0:t0 + 512], oo[gp][:], [f"oo{gp}"], [f"osb{1 + qg}"])

                ntl = len(tl)
                head_loads(0)
                if NH > 1:
                    head_loads(1)
                for i in range(ntl + 3):
                    if i < ntl:
                        s1(tl[i])
                    if 0 <= i - 1 < ntl:
                        s1b(tl[i - 1])
                    if 0 <= i - 2 < ntl:
                        s2(tl[i - 2])
                    if 0 <= i - 3 < ntl:
                        s3(tl[i - 3])
                        t = tl[i - 3]
                        if t["ki"] == 0 and t["qg"] == NTO and 1 <= t["h"] and t["h"] + 1 < NH:
                            head_loads(t["h"] + 1)
                S.barrier()
                S.emit()

        def phase_c1():
            S.barrier()
            with ExitStack() as st:
                def sb(n, shp, dt):
                    return st.enter_context(nc.sbuf_tensor(f"c1_{n}", shp, dt))
                ahg = sb("ahg", [128, 8, 512], BF16)
                asb = sb("asb", [128, 8, 512], BF16)
                gt = sb("gt", [128, 32, 512], BF16)
                yT = sb("yT", [128, DC, 512], BF16)
                wst = [sb(f"wst{i}", [128, 2048], F32) for i in range(3)]
                wp1 = [sb(f"wp1{i}", [128, 8 * 128], BF16) for i in range(2)]
                wp2 = [sb(f"wp2{i}", [128, 8 * 128], BF16) for i in range(2)]
                wo = [sb(f"wo{i}", [128, DC * 128], BF16) for i in range(2)]
                ta = [sb(f"ta{i}", [128, 512], F32) for i in range(2)]
                tb_ = [sb(f"tb{i}", [128, 512], F32) for i in range(2)]
                xs = [sb(f"xs{i}", [128, 512], F32) for i in range(2)]
                r = sb("r", [128, DC, 512], F32)
                sqb = [sb(f"sqb{i}", [128, 512], BF16) for i in range(2)]
                rhi = [sb(f"rhi{i}", [128, 512], BF16) for i in range(2)]
                rlo = [sb(f"rlo{i}", [128, 512], BF16) for i in range(2)]
                mean = sb("mean", [128, 512], F32)
                msq = sb("msq", [128, 512], F32)
                rstd = sb("rstd", [128, 512], F32)
                t1 = [sb(f"t1{i}", [128, 512], F32) for i in range(2)]
                yo = [sb(f"yo{i}", [128, 512], F32) for i in range(2)]
                yb = [sb(f"yb{i}", [128, 512], BF16) for i in range(2)]
                T = 512
                pending = []
                for ti, (o0, _) in enumerate(tilesC):
                    first = ti == 0
                    t0 = OWN0 + o0
                    ai = OWNT + ti
                    k.dma("sp", ahg[:], ohgT[:, :, t0:t0 + T].rearrange("h p t -> p h t"), [f"ohg{ai}"], ["ahg"])
                    k.dma("sp", asb[:], osbT[:, :, t0:t0 + T].rearrange("h p t -> p h t"), [f"osb{ai}"], ["asb"])
                    for c0 in range(0, 32, 8):
                        k.dma("sp", gt[:, c0:c0 + 8, :], gT[c0:c0 + 8, :, t0:t0 + T].rearrange("c p t -> p c t"), [f"gT{ai}"], ["gt"])
                    for c in range(DC):
                        s = c % 2
                        if pending:
                            pending.pop(0)()
                        fetch(wp1[s], f"wp1{s}", phg32[c], phgbf[c], f"phg{c}", first, wst, 8 * 128)
                        fetch(wp2[s], f"wp2{s}", psb32[c], psbbf[c], f"psb{c}", first, wst, 8 * 128)
                        for kc in range(8):
                            k.mm(ps[s][:, :], wp1[s][:, kc * 128:(kc + 1) * 128], ahg[:, kc, :], kc == 0, kc == 7,
                                 [f"wp1{s}", "ahg"], [f"ps{s}"], kc == 7)
                        for kc in range(8):
                            k.mm(ps[2 + s][:, :], wp2[s][:, kc * 128:(kc + 1) * 128], asb[:, kc, :], kc == 0, kc == 7,
                                 [f"wp2{s}", "asb"], [f"ps{2 + s}"], kc == 7)
                        k.tt("dve", ta[s][:], ps[s][:, :], gt[:, c, :], ALU.mult, [f"ps{s}", "gt"], [f"ta{s}"])
                        k.tt("dve", tb_[s][:], ps[2 + s][:, :], gt[:, 16 + c, :], ALU.mult, [f"ps{2 + s}", "gt"], [f"tb{s}"])
                        k.tt("dve", yT[:, c, :], ta[s][:], tb_[s][:], ALU.add, [f"ta{s}", f"tb{s}"], [f"yT{c}"])
                    while pending:
                        pending.pop(0)()

                    def stats(d):
                        z = d % 2
                        k.mm(ps[6][:, :], onesB, rhi[z][:], d == 0, False, [f"rhi{z}", "onesb"], ["ps6"], False)
                        k.mm(ps[6][:, :], onesB, rlo[z][:], False, d == DC - 1, [f"rlo{z}", "onesb"], ["ps6"], True)
                        k.mm(ps[7][:, :], onesB, sqb[z][:], d == 0, d == DC - 1, [f"sqb{z}", "onesb"], ["ps7"], True)

                    for c in range(DC):
                        s = c % 2
                        fetch(wo[s], f"wo{s}", wout32[c], woutbf[c], f"wout{c}", first, wst, DC * 128)
                        pm = ps[4 + s]
                        for kc in range(DC):
                            k.mm(pm[:, :], wo[s][:, kc * 128:(kc + 1) * 128], yT[:, kc, :], kc == 0, kc == DC - 1,
                                 [f"wo{s}", f"yT{kc}"], [f"ps{4 + s}"], kc == DC - 1)
                        if c > 0:
                            stats(c - 1)
                        k.dma("sp", xs[s][:], h1T[c, :, t0:t0 + T], [f"h1_{ai}_{c}_0"], [f"xs{s}"])
                        k.stt("dve", r[:, c, :], xs[s][:], ALPHA, pm[:, :], ALU.mult, ALU.add, [f"xs{s}", f"ps{4 + s}"], [f"r{c}"])
                        k.act(sqb[s][:], r[:, c, :], AF.Square, [f"r{c}"], [f"sqb{s}"])
                        k.cp("dve", rhi[s][:], r[:, c, :], [f"r{c}"], [f"rhi{s}"])
                        k.tt("dve", rlo[s][:], r[:, c, :], rhi[s][:], ALU.subtract, [f"r{c}", f"rhi{s}"], [f"rlo{s}"])
                    stats(DC - 1)
                    pending = ln_finish(r, T, LN_EPS, lnp_s[:, 2 * DC:3 * DC], lnp_s[:, 3 * DC:4 * DC], mean, msq, rstd, t1, yo, yb,
                                        lambda c, o0=o0: h2T[c, :, o0:o0 + T], lambda c, o0=o0: h2Tbf[c, :, o0:o0 + T],
                                        lambda c, b, ti=ti: f"h2_{ti}_{c}_{b}")
                while pending:
                    pending.pop(0)()
                S.barrier()
                S.emit()

        ffn_phase("fa", tilesA, xT, True, lambda ti, c, b: f"xin", xT, Wi["f1"], h1T, h1Tbf,
                  lambda ti, c, b: f"h1_{ti}_{c}_{b}", lnp_s[:, 0:DC], lnp_s[:, DC:2 * DC])
        phase_b1()
        phase_b2()
        phase_b3()
        phase_c1()
        ffn_phase("fc", tilesC, h2Tbf, False, lambda ti, c, b: f"h2_{ti}_{c}_{b}", h2T, Wi["f2"], outT, None,
                  lambda ti, c, b: f"out_{ti}_{c}", lnp_s[:, 4 * DC:5 * DC], lnp_s[:, 5 * DC:6 * DC])
        S.final_wait()
        S.emit()
    return nc


def _fm(W, kc, oc):
    return np.ascontiguousarray(W.reshape(kc, 128, oc, 128).transpose(2, 1, 0, 3)).reshape(oc, 128, kc * 128)


def _tm(W, kc, g):
    return np.ascontiguousarray(W.reshape(kc, 128, g, 512).transpose(2, 1, 0, 3)).reshape(g, 128, kc * 512)


def _consts():
    cA = np.zeros((128, 768), np.float32)
    cA[:, 0:128] = 1.0
    i = np.arange(64)
    cA[0:64, 128:192] = (i[:, None] > i[None, :])
    cA[0:64, 192:256] = (i[:, None] <= i[None, :])
    cA[0:64, 256:768] = np.tile((i[:, None] <= i[None, :]).astype(np.float32), (1, 8))
    cB = np.zeros((128, 4352), np.float32)
    p = np.arange(128)
    cB[:, 0:128] = -(p[:, None] >= p[None, :]).astype(np.float32)
    cB[:, 128:256] = np.eye(128, dtype=np.float32)
    t = np.arange(512)
    for j in range(4):
        m = ((j * 128 + p[:, None]) < t[None, :]).astype(np.float32)
        cB[:, 256 + 512 * j:768 + 512 * j] = m
        cB[:, 2304 + 512 * j:2816 + 512 * j] = -30000.0 * (1.0 - m)
    return cA, cB


def make_in_maps(inp, SEQ):
    f = lambda a: np.asarray(a, np.float32)
    x = f(inp["x"])
    B = x.shape[0]
    meta = f(inp["meta"])
    vec = lambda v: np.ascontiguousarray(f(v).reshape(-1, 128).T)
    lnp = np.concatenate([vec(inp[n][0]) for n in ("ln1_g", "ln1_b", "ln2_g", "ln2_b", "ln3_g", "ln3_b")], axis=1)
    w_in = f(inp["w_in"][0])
    fm_cols = np.concatenate([np.arange(0, 1024), np.arange(1024, 2048), np.arange(3072, 4096),
                              np.arange(4096, 5120), np.arange(5120, 6144), np.arange(7168, 11264)])
    tm_cols = np.concatenate([np.arange(1024, 2048), np.arange(2048, 3072), np.arange(6144, 7168)])
    lbl = f(inp["hg_lb_logits"])
    cA, cB = _consts()
    shared = {
        "constsA": cA, "constsB": cB,
        "lnp": np.ascontiguousarray(lnp),
        "bgate": vec(inp["b_gate"][0]),
        "gnorm": vec(inp["hg_norm_g"][0]),
        "lbl_fm": np.ascontiguousarray(np.concatenate([vec(lbl[0]), vec(lbl[1])], axis=1)),
        "lbl_tm": np.ascontiguousarray(np.broadcast_to(np.concatenate([lbl[0], lbl[1]])[None, :], (128, 2048))),
        "f1_g": _fm(f(inp["ffn1_w_gate"][0]), DC, FC), "f1_u": _fm(f(inp["ffn1_w_up"][0]), DC, FC),
        "f1_d": _fm(f(inp["ffn1_w_down"][0]), FC, DC),
        "f2_g": _fm(f(inp["ffn2_w_gate"][0]), DC, FC), "f2_u": _fm(f(inp["ffn2_w_up"][0]), DC, FC),
        "f2_d": _fm(f(inp["ffn2_w_down"][0]), FC, DC),
        "win_fm": _fm(np.ascontiguousarray(w_in[:, fm_cols]), DC, 72),
        "win_tm": _tm(np.ascontiguousarray(w_in[:, tm_cols]), DC, 6),
        "phg": _fm(f(inp["w_proj_hg"][0]), 8, DC), "psb": _fm(f(inp["w_proj_sb"][0]), 8, DC),
        "wout": _fm(f(inp["w_out"][0]), DC, DC),
    }
    maps = []
    L = SEQ + 128
    SO = SEQ // 2
    p = np.arange(128)
    for b in range(B):
        for j in range(2):
            xf = np.zeros((L, D), np.float32)
            if j == 1:
                xf[PADN:128] = meta
                xf[128:] = x[b]
                v0 = PADN
            else:
                xf[SO + PADN:SO + 128] = meta
                xf[SO + 128:] = x[b][:SO]
                v0 = SO + PADN
            valid = (np.arange(L) >= v0).astype(np.float32)
            m = dict(shared)
            m["xT"] = np.ascontiguousarray(xf.T).reshape(DC, 128, L)
            m["vm_tm"] = np.ascontiguousarray(valid.reshape(L // 128, 128).T)
            m["vm_fm"] = np.ascontiguousarray(np.broadcast_to(valid[None, :], (128, L)))
            maps.append(m)
    return maps


_CACHE = {}


def kernel(**inputs):
    x = np.asarray(inputs["x"])
    B, SEQ, _ = x.shape
    if SEQ not in _CACHE:
        _CACHE[SEQ] = build_program(SEQ)
    nc = _CACHE[SEQ]
    maps = make_in_maps(inputs, SEQ)
    res = run_bass_kernel_spmd(nc, maps, core_ids=list(range(2 * B)))
    out = np.empty((B, SEQ, D), np.float32)
    SO = SEQ // 2
    for b in range(B):
        for j in range(2):
            out[b, j * SO:(j + 1) * SO] = res.results[2 * b + j]["outT"].reshape(D, SO).T
    return out
```
